# Optimizing a Trainium2 kernel written in Bass

```python
import jax, jax.numpy as jnp
from jax import lax
import numpy as np

D_MODEL = 1024
BATCH = 4
SEQ = 8192
DEPTH = 2

GRID_W = 64
CTX_LEN = 256
CONV_WIDTH = D_MODEL
CONV_KERNEL = 31
N_HEADS = 16
N_KV_HEADS = 4
HEAD_DIM = 64
GROUP = N_HEADS // N_KV_HEADS
ATTN_WIDTH = N_HEADS * HEAD_DIM
KV_WIDTH = N_KV_HEADS * HEAD_DIM
Q_BLOCK = 128
ROPE_THETA = 10000.0
ROPE_AXIS_DIM = HEAD_DIM // 2
EPS = 1e-6
ATTN_SCALE = HEAD_DIM ** -0.5
IN_SPLITS = (2 * CONV_WIDTH, CONV_WIDTH, ATTN_WIDTH, KV_WIDTH, KV_WIDTH, ATTN_WIDTH, 2 * D_MODEL)
IN_WIDTH = sum(IN_SPLITS)
IN_OFFSETS = tuple(int(o) for o in np.cumsum(IN_SPLITS)[:-1])

kernel_name = "hybrid_conformer_gqa_prefix_dit_block"


def rms_norm(x, g):
    xf = x.astype(jnp.float32)
    y = xf * lax.rsqrt(jnp.mean(xf * xf, axis=-1, keepdims=True) + EPS)
    return (y * g.astype(jnp.float32)).astype(x.dtype)


def layer_norm(x, g, b):
    xf = x.astype(jnp.float32)
    mu = jnp.mean(xf, axis=-1, keepdims=True)
    var = jnp.mean(jnp.square(xf - mu), axis=-1, keepdims=True)
    y = (xf - mu) * lax.rsqrt(var + EPS)
    return (y * g.astype(jnp.float32) + b.astype(jnp.float32)).astype(x.dtype)


def axial_rope_tables(n_tokens):
    rows = n_tokens // GRID_W
    row = jnp.repeat(jnp.arange(rows, dtype=jnp.float32), GRID_W)
    col = jnp.tile(jnp.arange(GRID_W, dtype=jnp.float32), rows)
    inv_freq = ROPE_THETA ** (-jnp.arange(0, ROPE_AXIS_DIM, 2, dtype=jnp.float32) / ROPE_AXIS_DIM)
    ang = jnp.concatenate([row[:, None] * inv_freq, col[:, None] * inv_freq], axis=-1)
    return jnp.cos(ang), jnp.sin(ang)


def apply_rope(x, cos, sin):
    cos = cos.astype(x.dtype)[None, :, None, :]
    sin = sin.astype(x.dtype)[None, :, None, :]
    x1, x2 = jnp.split(x, 2, axis=-1)
    return jnp.concatenate([x1 * cos - x2 * sin, x2 * cos + x1 * sin], axis=-1)


def gqa_attend(qblk, k_all, v_all):
    s = jnp.einsum('bqkgd,bskd->bkgqs', qblk, k_all).astype(jnp.float32) * ATTN_SCALE
    p = jax.nn.softmax(s, axis=-1).astype(v_all.dtype)
    return jnp.einsum('bkgqs,bskd->bqkgd', p, v_all)


def latent_attention(q, k_all, v_all):
    b, n = q.shape[0], q.shape[1]
    nblk = n // Q_BLOCK
    qb = q.reshape(b, nblk, Q_BLOCK, N_KV_HEADS, GROUP, HEAD_DIM).transpose(1, 0, 2, 3, 4, 5)
    o = lax.map(lambda qblk: gqa_attend(qblk, k_all, v_all), qb)
    return o.transpose(1, 0, 2, 3, 4, 5).reshape(b, n, ATTN_WIDTH)


def conv_module(u, gate, conv_w, conv_b, ln_g, ln_b, w_conv_out):
    a, g = jnp.split(u, 2, axis=-1)
    y = a * jax.nn.sigmoid(g)
    pad = CONV_KERNEL // 2
    y = lax.conv_general_dilated(
        y, conv_w[:, None, :], window_strides=(1,), padding=[(pad, pad)],
        dimension_numbers=('NWC', 'WIO', 'NWC'), feature_group_count=CONV_WIDTH) + conv_b
    y = jax.nn.silu(layer_norm(y, ln_g, ln_b))
    y = y * jax.nn.silu(gate)
    return y @ w_conv_out


def split_proj(p):
    return jnp.split(p, IN_OFFSETS, axis=-1)


def merge_branches(y_conv, y_attn, gm, w_out):
    ga, gb = jnp.split(gm, 2, axis=-1)
    return (jax.nn.sigmoid(ga) * y_conv + jax.nn.sigmoid(gb) * y_attn) @ w_out


def setup_inputs(seed: int = 0) -> dict:
    key = jax.random.key(seed)
    ks = jax.random.split(key, 20)
    f32 = jnp.float32

    def nrm(k, shape, scale):
        return jax.random.normal(k, shape, f32) * scale

    L, D = DEPTH, D_MODEL
    return {
        "x": nrm(ks[0], (BATCH, SEQ, D), 1.0),
        "c": nrm(ks[1], (BATCH, D), 1.0),
        "ctx": nrm(ks[2], (BATCH, CTX_LEN, D), 1.0),
        "c_ctx": nrm(ks[3], (D,), 1.0),
        "w_mod": nrm(ks[4], (L, D, 3 * D), 0.5 * D ** -0.5),
        "b_mod": nrm(ks[5], (L, 3 * D), 0.02),
        "g_pre": 1.0 + nrm(ks[6], (L, D), 0.02),
        "g_post": 1.0 + nrm(ks[7], (L, D), 0.02),
        "w_in": nrm(ks[8], (L, D, IN_WIDTH), D ** -0.5),
        "conv_w": nrm(ks[9], (L, CONV_KERNEL, CONV_WIDTH), CONV_KERNEL ** -0.5),
        "conv_b": nrm(ks[10], (L, CONV_WIDTH), 0.02),
        "ln_g": 1.0 + nrm(ks[11], (L, CONV_WIDTH), 0.02),
        "ln_b": nrm(ks[12], (L, CONV_WIDTH), 0.02),
        "w_conv_out": nrm(ks[13], (L, CONV_WIDTH, D), CONV_WIDTH ** -0.5),
        "q_norm_g": 1.0 + nrm(ks[14], (L, HEAD_DIM), 0.02),
        "k_norm_g": 1.0 + nrm(ks[15], (L, HEAD_DIM), 0.02),
        "w_attn_out": nrm(ks[16], (L, ATTN_WIDTH, D), ATTN_WIDTH ** -0.5),
        "w_out": nrm(ks[17], (L, D, D), D ** -0.5),
    }


def reference(x, c, ctx, c_ctx, w_mod, b_mod, g_pre, g_post, w_in, conv_w, conv_b,
              ln_g, ln_b, w_conv_out, q_norm_g, k_norm_g, w_attn_out, w_out):
    b, n, _ = x.shape
    cos, sin = axial_rope_tables(n)

    for l in range(DEPTH):
        last = l == DEPTH - 1
        sh, sc, gt = jnp.split(jax.nn.silu(c) @ w_mod[l] + b_mod[l], 3, axis=-1)
        shc, scc, gtc = jnp.split(jax.nn.silu(c_ctx) @ w_mod[l] + b_mod[l], 3, axis=-1)

        h = rms_norm(x, g_pre[l]) * (1.0 + sc[:, None, :]) + sh[:, None, :]
        hc = rms_norm(ctx, g_pre[l]) * (1.0 + scc) + shc

        ua, gate_a, q, k, v, gate_b, gm = split_proj(h @ w_in[l])
        ua_c, gate_a_c, q_c, k_c, v_c, gate_b_c, gm_c = split_proj(hc @ w_in[l])

        q = apply_rope(rms_norm(q.reshape(b, n, N_HEADS, HEAD_DIM), q_norm_g[l]), cos, sin)
        k = apply_rope(rms_norm(k.reshape(b, n, N_KV_HEADS, HEAD_DIM), k_norm_g[l]), cos, sin)
        v = v.reshape(b, n, N_KV_HEADS, HEAD_DIM)
        k_c = rms_norm(k_c.reshape(b, CTX_LEN, N_KV_HEADS, HEAD_DIM), k_norm_g[l])
        v_c = v_c.reshape(b, CTX_LEN, N_KV_HEADS, HEAD_DIM)
        k_all = jnp.concatenate([k, k_c], axis=1)
        v_all = jnp.concatenate([v, v_c], axis=1)
        o = latent_attention(q, k_all, v_all)
        y_attn = (o * jax.nn.silu(gate_b)) @ w_attn_out[l]

        y_conv = conv_module(ua, gate_a, conv_w[l], conv_b[l], ln_g[l], ln_b[l], w_conv_out[l])

        out = merge_branches(y_conv, y_attn, gm, w_out[l])
        x_new = x + gt[:, None, :] * rms_norm(out, g_post[l])

        if not last:
            q_c = rms_norm(q_c.reshape(b, CTX_LEN, N_KV_HEADS, GROUP, HEAD_DIM), q_norm_g[l])
            o_c = gqa_attend(q_c, k_c, v_c).reshape(b, CTX_LEN, ATTN_WIDTH)
            y_attn_c = (o_c * jax.nn.silu(gate_b_c)) @ w_attn_out[l]
            y_conv_c = conv_module(ua_c, gate_a_c, conv_w[l], conv_b[l], ln_g[l], ln_b[l], w_conv_out[l])
            out_c = merge_branches(y_conv_c, y_attn_c, gm_c, w_out[l])
            ctx = ctx + gtc * rms_norm(out_c, g_post[l])
        x = x_new

    return x
```

```python
import contextlib
import types
import numpy as np
import concourse.bass as bass
import concourse.mybir as mybir
from concourse.bass_utils import run_bass_kernel_spmd

F32 = mybir.dt.float32
BF16 = mybir.dt.bfloat16
AF = mybir.ActivationFunctionType
ALU = mybir.AluOpType
AX = mybir.AxisListType

D = 1024
NB = 4
SEQ = 8192
HALF = 4096
CTX = 256
DEPTH = 2
NH = 16
NKV = 4
HD = 64
G = 256
HALO = 15
GE = G + 2 * HALO
NKT = 66
EPS = 1e-6
SCALE = 0.125
CK = 31
INW = 7680
O_UA, O_UG, O_GA, O_Q, O_K, O_V, O_GB, O_MA, O_MB = 0, 1024, 2048, 3072, 4096, 4352, 4608, 5632, 6656

ENGS = ("pe", "act", "dve", "pool", "sp")
NDMA = 8


class _Op:
    __slots__ = ("eng", "fn", "deps", "dma", "signaled", "tok", "prev_same_sem", "inc", "writes")

    def __init__(self, eng, fn, deps, dma, inc, writes=()):
        self.writes = tuple(writes)
        self.eng = eng
        self.fn = fn
        self.deps = deps
        self.dma = dma
        self.signaled = dma
        self.tok = None
        self.prev_same_sem = None
        self.inc = inc


class Prog:
    def __init__(self, nc, same_engine_sync=True):
        self.nc = nc
        self.ops = []
        self.last_w = {}
        self.readers = {}
        self.pending = {}
        self.same_engine_sync = same_engine_sync

    def transfer(self, old_keys, new_keys):
        dd = set()
        for k in old_keys:
            w = self.last_w.get(k)
            if w is not None:
                dd.add(w)
            rd = self.readers.get(k)
            if rd:
                dd.update(rd[0].values())
                dd.update(rd[1])
            if k in self.pending:
                dd.update(self.pending[k])
        for k in new_keys:
            self.pending.setdefault(k, set()).update(dd)

    @staticmethod
    def _freeze(fn):
        if not fn.__closure__:
            return fn
        cells = []
        for c in fn.__closure__:
            try:
                cells.append(types.CellType(c.cell_contents))
            except ValueError:
                cells.append(c)
        f2 = types.FunctionType(fn.__code__, fn.__globals__, fn.__name__, fn.__defaults__, tuple(cells))
        f2.__kwdefaults__ = fn.__kwdefaults__
        return f2

    def capture(self, f):
        self._capture = []
        f()
        ops, self._capture = self._capture, None
        return ops

    def replay(self, lists):
        n = max(len(x) for x in lists)
        for i in range(n):
            for x in lists:
                if i < len(x):
                    self.op(*x[i])

    def op(self, eng, fn, reads=(), writes=(), dma=False, inc=None):
        fn = self._freeze(fn)
        if getattr(self, "_capture", None) is not None:
            self._capture.append((eng, fn, tuple(reads), tuple(writes), dma, inc))
            return None
        idx = len(self.ops)
        deps = set()
        for r in reads:
            w = self.last_w.get(r)
            if w is not None:
                deps.add(w)
            if r in self.pending:
                deps.update(self.pending[r])
        for w_ in writes:
            w = self.last_w.get(w_)
            if w is not None:
                deps.add(w)
            rd = self.readers.get(w_)
            if rd:
                deps.update(rd[0].values())
                deps.update(rd[1])
            if w_ in self.pending:
                deps.update(self.pending.pop(w_))
        for r in reads:
            rd = self.readers.setdefault(r, ({}, []))
            if dma:
                rd[1].append(idx)
            else:
                rd[0][eng] = idx
        for w_ in writes:
            self.last_w[w_] = idx
            self.readers[w_] = ({}, [])
        deps.discard(idx)
        self.ops.append(_Op(eng, fn, deps, dma, inc if inc is not None else (16 if dma else 1), writes))
        return idx

    def pe(self, fn, reads=(), writes=()):
        return self.op("pe", fn, reads, writes)

    def act(self, fn, reads=(), writes=()):
        return self.op("act", fn, reads, writes)

    def dve(self, fn, reads=(), writes=()):
        return self.op("dve", fn, reads, writes)

    def pool(self, fn, reads=(), writes=()):
        return self.op("pool", fn, reads, writes)

    def dma(self, eng, fn, reads=(), writes=()):
        return self.op(eng, fn, reads, writes, dma=True)

    def _skip(self, od, o):
        return (not od.dma) and (not o.dma) and od.eng == o.eng and (od.eng == "pe" or not self.same_engine_sync)

    def emit(self, final_wait_ops=()):
        nc = self.nc
        ops = self.ops
        for o in ops:
            for d in o.deps:
                od = ops[d]
                if od.dma or self._skip(od, o):
                    continue
                od.signaled = True
        for d in final_wait_ops:
            ops[d].signaled = True
        with contextlib.ExitStack() as st:
            esem = {e: st.enter_context(nc.semaphore("s_" + e)) for e in ENGS}
            dsem = {e: [st.enter_context(nc.semaphore("d_%s%d" % (e, i))) for i in range(NDMA)]
                    for e in ("sp", "pool", "act")}
            ecount = {e: 0 for e in ENGS}
            dcount = {e: 0 for e in dsem}
            dlast = {e: [None] * NDMA for e in dsem}
            dval = {e: [0] * NDMA for e in dsem}
            ccsem = st.enter_context(nc.semaphore("s_cc"))
            cccount = 0
            for i, o in enumerate(ops):
                if o.dma and o.inc == 1:
                    cccount += 1
                    o.tok = (ccsem, cccount, ("cc",))
                elif o.dma:
                    n = dcount[o.eng]
                    dcount[o.eng] += 1
                    slot = n % NDMA
                    dval[o.eng][slot] += o.inc
                    o.tok = (dsem[o.eng][slot], dval[o.eng][slot], ("d", o.eng, slot))
                    o.prev_same_sem = dlast[o.eng][slot]
                    dlast[o.eng][slot] = i
                elif o.signaled:
                    ecount[o.eng] += 1
                    o.tok = (esem[o.eng], ecount[o.eng], ("e", o.eng))
            block = st.enter_context(nc.Block())
            handles = {"pe": nc.tensor, "act": nc.scalar, "dve": nc.vector,
                       "pool": nc.gpsimd, "sp": nc.sync}

            def stream(e):
                h = handles[e]
                waited = {}

                def wait_tok(tok):
                    sem, val, key = tok
                    if waited.get(key, 0) >= val:
                        return
                    waited[key] = val
                    h.wait_ge(sem, val)

                for i, o in enumerate(ops):
                    if o.eng != e:
                        continue
                    for d in sorted(o.deps):
                        od = ops[d]
                        if od.tok is None or self._skip(od, o):
                            continue
                        wait_tok(od.tok)
                    if o.dma and o.prev_same_sem is not None:
                        wait_tok(ops[o.prev_same_sem].tok)
                    ins = o.fn(h)
                    if o.dma and o.inc == 1:
                        ins.then_inc(o.tok[0])
                    elif o.dma:
                        ins.then_inc(o.tok[0], o.inc)
                    elif o.signaled:
                        ins.then_inc(o.tok[0], 1)
                if e == "sp":
                    for d in final_wait_ops:
                        wait_tok(ops[d].tok)

            block.tensor(lambda h: stream("pe"))
            block.scalar(lambda h: stream("act"))
            block.vector(lambda h: stream("dve"))
            block.gpsimd(lambda h: stream("pool"))
            block.sync(lambda h: stream("sp"))
        return ecount, dcount


class Builder:
    def __init__(self, layers, b_groups, a_tiles, ctx_update_layers, out_rows):
        self.layers = layers
        self.b_groups = b_groups
        self.a_tiles = a_tiles
        self.ctx_update_layers = ctx_update_layers
        self.out_rows = out_rows
        self.nc = bass.Bass("TRN2", target_bir_lowering=False)
        self.P = Prog(self.nc)
        self.final_ops = []
        self._rr = {}

    def dram_in(self, name, shape, dt=F32):
        return self.nc.dram_tensor(name, list(shape), dt, kind="ExternalInput").ap()

    def dram_out(self, name, shape, dt=F32):
        return self.nc.dram_tensor(name, list(shape), dt, kind="ExternalOutput").ap()

    def dram_tmp(self, name, shape, dt=F32):
        return self.nc.dram_tensor(name, list(shape), dt, kind="Internal").ap()

    def sb(self, name, shape, dt):
        return self.st.enter_context(self.nc.sbuf_tensor(name, list(shape), dt))

    def rr(self, pool, n):
        i = self._rr.get(pool, 0)
        self._rr[pool] = i + 1
        return i % n

    def bank(self, pool):
        banks = self.pools[pool]
        return banks[self.rr(("bank", pool), len(banks))]

    def build(self):
        nc = self.nc
        P = self.P
        L = DEPTH
        self.x_in = self.dram_in("x", [SEQ, D])
        self.ctx_in = self.dram_in("ctx", [CTX, D])
        self.cvec = self.dram_in("cvec", [2, D])
        self.w_mod = self.dram_in("w_mod", [L, D, 3 * D])
        self.b_mod = self.dram_in("b_mod", [L, 3 * D])
        self.g_pre = self.dram_in("g_pre", [L, D])
        self.g_post = self.dram_in("g_post", [L, D])
        self.w_in = self.dram_in("w_in", [L, D, INW])
        self.conv_w = self.dram_in("conv_w", [L, CK, D])
        self.conv_b = self.dram_in("conv_b", [L, D])
        self.ln_g = self.dram_in("ln_g", [L, D])
        self.ln_b = self.dram_in("ln_b", [L, D])
        self.w_co = self.dram_in("w_conv_out", [L, D, D])
        self.q_g = self.dram_in("q_norm_g", [L, HD])
        self.k_g = self.dram_in("k_norm_g", [L, HD])
        self.w_ao = self.dram_in("w_attn_out", [L, D, D])
        self.w_o = self.dram_in("w_out", [L, D, D])
        self.rope = self.dram_in("rope", [SEQ, HD])
        self.masks = self.dram_in("masks", [128, 2])
        self.out = self.dram_out("out", [self.out_rows, D])
        last = self.layers[-1]
        self.ctx_out = None
        if last in self.ctx_update_layers:
            self.ctx_out = self.dram_out("ctx_out", [CTX, D])
        self.x_mid = None
        if len(self.layers) > 1:
            self.x_mid = self.dram_tmp("x_mid", [SEQ, D])
            self.ctx_mid = self.dram_tmp("ctx_mid", [CTX, D])
        if USE_CC:
            self.cc_in_k = {l: self.dram_tmp("cc_in_k%d" % l, [256, HALF], BF16) for l in self.layers}
            self.cc_out_k = {l: self.dram_tmp("cc_out_k%d" % l, [512, HALF], BF16) for l in self.layers}
            self.cc_in_v = {(l, j): self.dram_tmp("cc_in_v%d_%d" % (l, j), [128 * 16, 288], BF16)
                            for l in self.layers for j in range(2)}
            self.cc_out_v = {(l, j): self.dram_tmp("cc_out_v%d_%d" % (l, j), [2 * 128 * 16, 288], BF16)
                             for l in self.layers for j in range(2)}
            self.cc_in_h = self.dram_tmp("cc_in_h", [32, D], F32)
            self.cc_out_h = self.dram_tmp("cc_out_h", [64, D], F32)
        self.wb_in = {l: self.dram_tmp("wb_in%d" % l, [D, INW], BF16) for l in self.layers}
        self.wb_co = {l: self.dram_tmp("wb_co%d" % l, [D, D], BF16) for l in self.layers}
        self.wb_ao = {l: self.dram_tmp("wb_ao%d" % l, [D, D], BF16) for l in self.layers}
        self.wb_o = {l: self.dram_tmp("wb_o%d" % l, [D, D], BF16) for l in self.layers}

        with contextlib.ExitStack() as st:
            self.st = st
            sb = self.sb
            self.KT = sb("KT", [128, 2, NKT * 128], BF16)
            self.Vflat = sb("V", [128, NKT * NKV * (HD + 1) + 64], BF16)
            self.V = self.Vflat[:, 0:NKT * NKV * (HD + 1)].rearrange("p (t h d) -> p t h d", t=NKT, h=NKV)
            self.identf = sb("identf", [128, 128], F32)
            self.identb = sb("identb", [128, 128], BF16)
            self.onesf = sb("onesf", [128, 128], F32)
            self.onesb = sb("onesb", [128, 128], BF16)
            self.sel = sb("sel", [128, 128], F32)
            self.mask_sb = sb("mask_sb", [128, 2], F32)
            self.pcol = sb("pcol", [128, 8, 16], F32)
            self.cw = sb("cw", [128, 8, 32], F32)
            self.csl = sb("csl", [128, 8, 2], F32)
            self.modc = sb("modc", [128, 3, 8, 2], F32)
            self.AT = sb("AT", [128, 8, 2], F32)
            self.gcol = sb("gcol", [128, 8, 2], F32)
            self.gtg = sb("gtg", [128, 2, D], F32)
            self.dg = sb("dg", [128, 2, 128], F32)
            self.gq = sb("gq", [128, HD], F32)
            self.gk = sb("gk", [128, HD], F32)
            self.NW = 4
            self.wsl = [sb("wsl%d" % i, [128, 8, 512], BF16) for i in range(self.NW)]
            self.xt = [sb("xt%d" % i, [128, D], F32) for i in range(2)]
            self.junk = sb("junk", [128, D], BF16)
            self.small = sb("small", [128, 64], F32)
            self.ropet = [sb("ropet%d" % i, [128, HD], F32) for i in range(2)]
            self.hTa = [sb("hTa%d" % i, [128, 8, 128], BF16) for i in range(2)]
            self.hTs = [sb("hT%d" % i, [128, 8, GE], BF16) for i in range(2)]
            self.m1 = sb("m1", [128, 8, G], BF16)
            self.qf = [sb("qf%d" % i, [128, 512], F32) for i in range(2)]
            self.qa = [sb("qa%d" % i, [128, 512], F32) for i in range(2)]
            self.krot = [sb("krot%d" % i, [128, 256], BF16) for i in range(2)]
            self.hn_stats = sb("hn_stats", [128, 2, 3, 8], F32)
            self.QT_t = sb("QT", [128, 4, 2, 4, 128], BF16)
            self.QT = self.QT_t[:]
            self.qrot_t = [sb("qrot%d" % i, [128, 2, 4, 2, 64], BF16) for i in range(2)]
            self.qnat = [sb("qnat%d" % i, [128, 512], BF16) for i in range(2)]
            self.otmp = [sb("otmp%d" % i, [128, 512], F32) for i in range(2)]
            UB = 40 * 1024
            self.U = sb("U", [128, UB // 2], BF16)
            self.carve()
            self.psum = [st.enter_context(nc.psum_tensor("ps%d" % i, [128, 512], F32)) for i in range(8)]
            self.pools = {"a": [0, 1], "b": [2, 3], "s": [4, 5], "o": [6], "m": [7], "pa_t": [0, 1, 2, 3], "pa_k": [4, 5, 6, 7], "q4": [0, 1, 2, 3], "a1": [0]}

            self.emit_consts()
            self.emit_casts(self.layers[0])
            for l in self.layers:
                self.emit_layer(l)
            fin_names = [self.out.name] + ([self.ctx_out.name] if self.ctx_out is not None else [])
            self.final_ops = [i for i, o in enumerate(P.ops)
                              if any(isinstance(w, tuple) and len(w) == 3 and w[0] == "dram" and w[1] in fin_names
                                     for w in o.writes)]
            counts = P.emit(final_wait_ops=self.final_ops)
        return counts

    def carve(self):
        U = self.U
        off = [0]

        def take(nbytes):
            o = off[0]
            off[0] += (nbytes + 63) // 64 * 64 // 2
            return o

        def vb(nel):
            o = take(nel * 2)
            return U[:, o:o + nel]

        def vf(nel):
            o = take(nel * 4)
            return U[:, o:o + 2 * nel].bitcast(F32)

        self.yT = vb(8 * GE).rearrange("p (c t) -> p c t", c=8)
        self.sg = [vf(GE), vf(GE)]
        self.ycb = vb(8 * G).rearrange("p (c t) -> p c t", c=8)
        self.sq = [vb(G), vb(G)]
        self.mean = vf(G)
        self.rstd = vf(G)
        self.nmr = vf(G)
        self.ztmp = [vf(G), vf(G)]
        self.uT = vb(8 * G).rearrange("p (c t) -> p c t", c=8)
        self.gat = [vb(G), vb(G)]
        self.diag = [vb(CK * 128).rearrange("p (j m) -> p j m", j=CK) for _ in range(2)]
        conv_end = off[0]
        off[0] = 0
        self.og = vb(16 * G).rearrange("p (h t) -> p h t", h=16)
        self.gbT = [vb(4 * G).rearrange("p (j t) -> p j t", j=4) for _ in range(2)]
        self.PT = [vb(512) for _ in range(4)]
        self.oaug = [vf(512) for _ in range(2)]
        self.rbc = vf(512)
        self.sgb = [vf(G), vf(G)]
        attn_end = off[0]
        assert max(conv_end, attn_end) <= self.U.shape[1], (conv_end, attn_end, self.U.shape)
        self.conv_keys = ["yT", "sg0", "sg1", "ycb", "sq0", "sq1", "mean", "rstd", "nmr", "ztmp0", "ztmp1",
                          "uT", "gat0", "gat1", "diag0", "diag1"]
        self.attn_keys = ["og", "gbT0", "gbT1", "PT0", "PT1", "PT2", "PT3", "oaug0", "oaug1", "rbc", "sgb0", "sgb1"]

    def emit_consts(self):
        P = self.P
        identf, identb, onesf, onesb, sel = self.identf, self.identb, self.onesf, self.onesb, self.sel
        P.pool(lambda h: h.memset(identf[:], 1.0), writes=["identf"])
        P.pool(lambda h: h.affine_select(out=identf[:], in_=identf[:], pattern=[[-1, 128]],
                                          compare_op=ALU.is_equal, fill=0.0, base=0, channel_multiplier=1),
               reads=["identf"], writes=["identf"])
        P.dve(lambda h: h.tensor_copy(out=identb[:], in_=identf[:]), reads=["identf"], writes=["identb"])
        P.pool(lambda h: h.memset(onesf[:], 1.0), writes=["onesf"])
        P.pool(lambda h: h.memset(onesb[:], 1.0), writes=["onesb"])
        P.pool(lambda h: h.memset(sel[:], 1.0), writes=["sel"])
        P.pool(lambda h: h.affine_select(out=sel[:], in_=sel[:], pattern=[[0, 128]],
                                          compare_op=ALU.is_equal, fill=0.0, base=-64, channel_multiplier=1),
               reads=["sel"], writes=["sel"])
        V = self.V
        P.pool(lambda h: h.memset(V[:, :, :, HD:HD + 1], 1.0), writes=["Vones"])
        P.pool(lambda h: h.memset(self.Vflat[:, NKT * NKV * (HD + 1):], 0.0), writes=["Vpad"])
        P.dma("sp", lambda h: h.dma_start(out=self.mask_sb[:], in_=self.masks), writes=["mask_sb"])
        QT = self.QT
        P.pool(lambda h: h.memset(QT[64:128, 0::2, :, :, :], 0.0), writes=["QT"])
        P.pool(lambda h: h.memset(QT[0:64, 1::2, :, :, :], 0.0), writes=["QT"])

    def wslot(self):
        i = self.rr("wslot", self.NW)
        return i, self.wsl[i], ("wsl", i)

    def load_w_block(self, src2d, col0, ncols=512):
        i, t, key = self.wslot()
        src = src2d[:, col0:col0 + ncols].rearrange("(k p) c -> p k c", p=128)
        self.P.dma("sp", lambda h: h.dma_start(out=t[:, :, 0:ncols], in_=src), reads=[("wb", src2d.name, col0 // 512)], writes=[key])
        return t, key

    def emit_casts(self, l, deferred=False):
        P = self.P
        wi, wbi = self.w_in[l], self.wb_in[l]
        order = [(wi, wbi, O_K), (wi, wbi, O_Q), (wi, wbi, O_Q + 512),
                 (wi, wbi, O_UG), (wi, wbi, O_UA), (wi, wbi, O_UG + 512), (wi, wbi, O_UA + 512),
                 (wi, wbi, O_GA), (wi, wbi, O_GA + 512),
                 (self.w_co[l], self.wb_co[l], 0), (wi, wbi, O_MA), (self.w_co[l], self.wb_co[l], 512), (wi, wbi, O_MA + 512),
                 (wi, wbi, O_GB), (wi, wbi, O_GB + 512),
                 (self.w_ao[l], self.wb_ao[l], 0), (wi, wbi, O_MB), (self.w_ao[l], self.wb_ao[l], 512), (wi, wbi, O_MB + 512),
                 (self.w_o[l], self.wb_o[l], 0), (self.w_o[l], self.wb_o[l], 512)]
        todo = []
        for src, dst, c0 in order:
            todo.append(lambda src=src, dst=dst, c0=c0: P.dma(
                "pool", lambda h: h.dma_start(out=dst[:, c0:c0 + 512], in_=src[:, c0:c0 + 512]),
                writes=[("wb", dst.name, c0 // 512)]))
        if deferred:
            return todo
        for f in todo:
            f()

    def emit_layer(self, l):
        P = self.P
        nc = self.nc
        first = (l == self.layers[0])
        lastl = (l == self.layers[-1])
        x_src = self.x_in if first else self.x_mid
        ctx_src = self.ctx_in if first else self.ctx_mid
        if lastl:
            x_dst, ctx_dst = self.out, self.ctx_out
        else:
            x_dst, ctx_dst = self.x_mid, self.ctx_mid
        self.cur = dict(l=l, x_src=x_src, ctx_src=ctx_src, x_dst=x_dst, ctx_dst=ctx_dst)

        prow = self.xt[1]
        rows = [self.cvec[0:1, :], self.cvec[1:2, :], self.g_pre[l:l + 1, :], self.g_post[l:l + 1, :],
                self.conv_b[l:l + 1, :], self.ln_g[l:l + 1, :], self.ln_b[l:l + 1, :],
                self.b_mod[l:l + 1, 0:D], self.b_mod[l:l + 1, D:2 * D], self.b_mod[l:l + 1, 2 * D:3 * D]]
        for r, src in enumerate(rows):
            P.dma("sp", lambda h, r=r, src=src: h.dma_start(out=prow[r:r + 1, :], in_=src), writes=["xt1"])
        cwrow = self.xt[0]
        P.dma("sp", lambda h: h.dma_start(out=cwrow[0:CK, :], in_=self.conv_w[l]), writes=["xt0"])
        pb = self.bank("m")
        ps = self.psum[pb]
        for k in range(8):
            P.pe(lambda h, k=k: h.transpose(out=ps[:, k * 16:k * 16 + 10], in_=prow[0:10, k * 128:(k + 1) * 128],
                                            identity=self.identf[0:10, 0:10]),
                 reads=["xt1", "identf"], writes=[("ps", pb)])
        pcol = self.pcol
        P.dve(lambda h: h.tensor_copy(out=pcol[:, :, 0:10], in_=ps[:, 0:128].rearrange("p (k r) -> p k r", k=8)[:, :, 0:10]),
              reads=[("ps", pb)], writes=["pcol"])
        pb2 = self.bank("a")
        ps2 = self.psum[pb2]
        for k in range(8):
            P.pe(lambda h, k=k: h.transpose(out=ps2[:, k * 32:k * 32 + CK], in_=cwrow[0:CK, k * 128:(k + 1) * 128],
                                            identity=self.identf[0:CK, 0:CK]),
                 reads=["xt0", "identf"], writes=[("ps", pb2)])
        cw = self.cw
        P.dve(lambda h: h.tensor_copy(out=cw[:, :, 0:CK], in_=ps2[:, 0:256].rearrange("p (k r) -> p k r", k=8)[:, :, 0:CK]),
              reads=[("ps", pb2)], writes=["cw"])
        csl = self.csl
        P.act(lambda h: h.activation(out=csl[:], in_=pcol[:, :, 0:2], func=AF.Silu), reads=["pcol"], writes=["csl"])
        pm_b = self.bank("m")
        pm = self.psum[pm_b]
        for blk in range(12):
            i, t, key = self.wslot()
            tf = t[:].rearrange("p k c -> p (k c)").bitcast(F32).rearrange("p (k c) -> p k c", k=8)
            src = self.w_mod[l][:, blk * 256:(blk + 1) * 256].rearrange("(k p) c -> p k c", p=128)
            P.dma("sp", lambda h, tf=tf, src=src: h.dma_start(out=tf, in_=src), writes=[key])
            for f2 in range(2):
                fc = blk * 2 + f2
                for k in range(8):
                    P.pe(lambda h, tf=tf, f2=f2, k=k, fc=fc: h.matmul(pm[:, fc * 2:fc * 2 + 2],
                                                                      lhsT=tf[:, k, f2 * 128:(f2 + 1) * 128],
                                                                      rhs=csl[:, k, :], start=(k == 0), stop=(k == 7)),
                         reads=[key, "csl"], writes=[("ps", pm_b)])
        modc = self.modc
        bm = pcol[:, :, 7:10].rearrange("p f r -> p r f").unsqueeze(3).to_broadcast([128, 3, 8, 2])
        P.dve(lambda h: h.tensor_tensor(out=modc[:], in0=pm[:, 0:48].rearrange("p (r f s) -> p r f s", r=3, f=8),
                                        in1=bm, op=ALU.add),
              reads=[("ps", pm_b), "pcol"], writes=["modc"])
        AT, gcol = self.AT, self.gcol
        P.dve(lambda h: h.tensor_scalar(out=AT[:], in0=modc[:, 1, :, :], scalar1=1.0, scalar2=None, op0=ALU.add),
              reads=["modc"], writes=["AT"])
        P.dve(lambda h: h.tensor_tensor(out=AT[:], in0=AT[:], in1=pcol[:, :, 2:3].to_broadcast([128, 8, 2]), op=ALU.mult),
              reads=["AT", "pcol"], writes=["AT"])
        P.dve(lambda h: h.tensor_tensor(out=gcol[:], in0=modc[:, 2, :, :], in1=pcol[:, :, 3:4].to_broadcast([128, 8, 2]),
                                        op=ALU.mult),
              reads=["modc", "pcol"], writes=["gcol"])
        for s in range(2):
            for half in range(2):
                gb_ = self.bank("b")
                pg = self.psum[gb_]
                for kk in range(4):
                    k = half * 4 + kk
                    di = self.rr("dg", 2)
                    dgt = self.dg[:, di, :]
                    P.dve(lambda h, dgt=dgt, k=k, s=s: h.tensor_scalar(out=dgt, in0=self.identf[:], scalar1=gcol[:, k, s:s + 1],
                                                                       scalar2=None, op0=ALU.mult),
                          reads=["identf", "gcol"], writes=[("dg", di)])
                    P.pe(lambda h, dgt=dgt, kk=kk: h.matmul(pg[:, kk * 128:(kk + 1) * 128], lhsT=self.onesf[:], rhs=dgt,
                                                            start=True, stop=True),
                         reads=[("dg", di), "onesf"], writes=[("ps", gb_)])
                P.act(lambda h, s=s, half=half, pg=pg: h.copy(out=self.gtg[:, s, half * 512:(half + 1) * 512], in_=pg[:]),
                      reads=[("ps", gb_)], writes=["gtg"])
        P.dma("sp", lambda h: h.dma_start(out=self.gq[:], in_=self.q_g[l].partition_broadcast(128)), writes=["gq"])
        P.dma("sp", lambda h: h.dma_start(out=self.gk[:], in_=self.k_g[l].partition_broadcast(128)), writes=["gk"])

        self.wkv, self.wkv_key = self.load_w_block(self.wb_in[l], O_K, 512)
        jobs = []
        for kt in self.a_tiles[l]:
            jobs.append(lambda kt=kt: self.emit_A_tile(kt, x_src[kt * 128:(kt + 1) * 128, :], 0,
                                                       self.rope[kt * 128:(kt + 1) * 128, :],
                                                       src_keys=[("dram", x_src.name, kt * 128)]))
        for ci in range(2):
            jobs.append(lambda ci=ci: self.emit_A_tile(64 + ci, ctx_src[ci * 128:(ci + 1) * 128, :], 1, None,
                                                       src_keys=[("dram", ctx_src.name, ci * 128)]))
        for j0 in range(0, len(jobs), 2):
            P.replay([P.capture(j) for j in jobs[j0:j0 + 2]])
        if USE_CC:
            self.emit_kv_exchange(l)
        self.cur["halo_override"] = {}
        if USE_CC and not first:
            ng2 = len(self.b_groups[l])
            self.cur["halo_override"] = {(0, "L"): (self.cc_out_h[17:32, :], "cc_out_h"),
                                         (ng2 - 1, "R"): (self.cc_out_h[32:47, :], "cc_out_h")}
        items = [(g, False) for g in self.b_groups[l]]
        if l in self.ctx_update_layers:
            items.append((0, True))
        for st1 in self.emit_B_hT(items[0][0], items[0][1], 0):
            st1()()
        cast_todo = [] if lastl else self.emit_casts(self.layers[self.layers.index(l) + 1], deferred=True)
        for i_, (g, isc) in enumerate(items):
            nxt = None
            if i_ + 1 < len(items):
                nxt = self.emit_B_hT(items[i_ + 1][0], items[i_ + 1][1], (i_ + 1) % 2, pp="a1")
            self.emit_B_group(g, isc, i_ % 2, nxt)
            ncast = len(cast_todo) if i_ == len(items) - 1 else min(2, len(cast_todo))
            for _ in range(ncast):
                cast_todo.pop(0)()
        if USE_CC and not lastl:
            self.emit_halo_exchange()

    def emit_kv_exchange(self, l):
        P = self.P
        KT, V = self.KT, self.V
        nt = HALF // 128
        VW = NKV * (HD + 1)
        cik, cok = self.cc_in_k[l], self.cc_out_k[l]
        kkeys = [("KT", kt) for kt in range(nt)]
        P.dma("sp", lambda h: h.dma_start(out=cik.rearrange("(p a) k -> p a k", a=2), in_=KT[:, :, 0:HALF]),
              reads=kkeys, writes=[("cc", cik.name)])
        for j in range(2):
            civ = self.cc_in_v[(l, j)]
            P.dma("sp", lambda h: h.dma_start(out=civ.rearrange("(p t) c -> p t c", t=16)[:, :, 0:VW],
                                              in_=V[:, j * 16:(j + 1) * 16, :, :].rearrange("p t h d -> p t (h d)")),
                  reads=[("V", j * 16 + kt) for kt in range(16)] + ["Vones"], writes=[("cc", civ.name)])
        P.op("pool", lambda h: h.collective_compute("AllGather", ALU.bypass, replica_groups=RG_PAIRS, ins=[cik], outs=[cok]),
             reads=[("cc", cik.name)], writes=[("cc", cok.name)], dma=True, inc=1)
        for j in range(2):
            civ, cov = self.cc_in_v[(l, j)], self.cc_out_v[(l, j)]
            P.op("pool", lambda h: h.collective_compute("AllGather", ALU.bypass, replica_groups=RG_PAIRS, ins=[civ], outs=[cov]),
                 reads=[("cc", civ.name)], writes=[("cc", cov.name)], dma=True, inc=1)
        for r in range(2):
            P.dma("sp", lambda h: h.dma_start(out=KT[:, :, r * HALF:(r + 1) * HALF],
                                              in_=cok[r * 256:(r + 1) * 256, :].rearrange("(p a) k -> p a k", a=2)),
                  reads=[("cc", cok.name)], writes=[("KT", r * nt + kt) for kt in range(nt)])
            for j in range(2):
                cov = self.cc_out_v[(l, j)]
                t0_ = r * nt + j * 16
                P.dma("sp", lambda h: h.dma_start(out=V[:, t0_:t0_ + 16, :, :].rearrange("p t h d -> p t (h d)"),
                                                  in_=cov[r * 2048:(r + 1) * 2048, :].rearrange("(p t) c -> p t c", t=16)[:, :, 0:VW]),
                      reads=[("cc", cov.name)], writes=[("V", t0_ + kt) for kt in range(16)])

    def emit_halo_exchange(self):
        P = self.P
        xm = self.x_mid
        cih, coh = self.cc_in_h, self.cc_out_h
        P.dma("sp", lambda h: h.dma_start(out=cih[0:16, :], in_=xm[0:16, :]),
              reads=[("dram", xm.name, 0)], writes=["cc_in_h"])
        P.dma("sp", lambda h: h.dma_start(out=cih[16:32, :], in_=xm[HALF - 16:HALF, :]),
              reads=[("dram", xm.name, HALF - 128)], writes=["cc_in_h"])
        P.op("pool", lambda h: h.collective_compute("AllGather", ALU.bypass, replica_groups=RG_PAIRS, ins=[cih], outs=[coh]),
             reads=["cc_in_h"], writes=["cc_out_h"], dma=True, inc=1)

    def emit_hT(self, src_rows, nrows, mset, dst, dst_key, row_dmas=None, src_keys=(), dma_eng="pool", pp="b", two_stage=False):
        P = self.P
        xi = self.rr("xt", 2)
        xt = self.xt[xi]
        xk = "xt%d" % xi
        if row_dmas is None:
            P.dma(dma_eng, lambda h: h.dma_start(out=xt[0:nrows, :], in_=src_rows), reads=list(src_keys), writes=[xk])
        else:
            P.pool(lambda h: h.memset(xt[0:nrows, :], 0.0), writes=[xk])
            for (r0, n, src) in row_dmas:
                P.dma("pool", lambda h, r0=r0, n=n, src=src: h.dma_start(out=xt[r0:r0 + n, :], in_=src), reads=list(src_keys), writes=[xk])
        si = self.rr("small", 8)
        ss = self.small[:, si * 4:si * 4 + 1]
        ln = self.small[:, si * 4 + 1:si * 4 + 2]
        rs = self.small[:, si * 4 + 2:si * 4 + 3]
        sk = ("small", si)
        junk = self.junk
        P.dve(lambda h: h.scalar_tensor_tensor(out=junk[0:nrows, :], in0=xt[0:nrows, :], scalar=1.0, in1=xt[0:nrows, :],
                                               op0=ALU.mult, op1=ALU.mult, accum_out=ss[0:nrows, :]),
              reads=[xk], writes=["junk", sk])
        P.act(lambda h: h.activation(out=ln[0:nrows, :], in_=ss[0:nrows, :], func=AF.Ln, scale=1.0 / D, bias=EPS),
              reads=[sk], writes=[sk])
        P.act(lambda h: h.activation(out=rs[0:nrows, :], in_=ln[0:nrows, :], func=AF.Exp, scale=-0.5),
              reads=[sk], writes=[sk])
        P.dve(lambda h: h.tensor_scalar(out=xt[0:nrows, :], in0=xt[0:nrows, :], scalar1=rs[0:nrows, :], scalar2=None,
                                        op0=ALU.mult),
              reads=[xk, sk], writes=[xk])
        if two_stage:
            return lambda: self.emit_hT_stage2(xt, xk, nrows, mset, dst, dst_key, pp)
        self.emit_hT_stage2(xt, xk, nrows, mset, dst, dst_key, pp)

    def emit_hT_stage2(self, xt, xk, nrows, mset, dst, dst_key, pp):
        P = self.P
        for half in range(2):
            b_ = self.bank(pp)
            pt = self.psum[b_]
            for kk in range(4):
                k = half * 4 + kk
                P.pe(lambda h, k=k, kk=kk, pt=pt: h.transpose(out=pt[:, kk * 128:kk * 128 + nrows],
                                                              in_=xt[0:nrows, k * 128:(k + 1) * 128],
                                                              identity=self.identf[0:nrows, 0:nrows]),
                     reads=[xk, "identf"], writes=[("ps", b_)])
            for kk in range(4):
                k = half * 4 + kk
                if kk % 2 == 0 and pp != "a1":
                    P.act(lambda h, k=k, kk=kk, pt=pt: h.activation(out=dst[:, k, :], in_=pt[:, kk * 128:kk * 128 + nrows],
                                                                    func=AF.Identity, scale=self.AT[:, k, mset:mset + 1],
                                                                    bias=self.modc[:, 0, k, mset:mset + 1]),
                          reads=[("ps", b_), "AT", "modc"], writes=[dst_key])
                else:
                    P.dve(lambda h, k=k, kk=kk, pt=pt: h.tensor_scalar(out=dst[:, k, :], in0=pt[:, kk * 128:kk * 128 + nrows],
                                                                       scalar1=self.AT[:, k, mset:mset + 1],
                                                                       scalar2=self.modc[:, 0, k, mset:mset + 1],
                                                                       op0=ALU.mult, op1=ALU.add),
                          reads=[("ps", b_), "AT", "modc"], writes=[dst_key])

    def emit_headnorm(self, src, src_key, nh, gain, gain_key, ropet, rope_key, dst, dst_key):
        P = self.P
        n = nh * HD
        bi = self.rr("qfbuf", 2)
        qf = self.qf[bi][:, 0:n]
        qa = self.qa[bi][:, 0:n]
        KF, KA, KA2 = "qf%d" % bi, "qa%d" % bi, "qa2_%d" % bi
        qf3 = qf.rearrange("p (h d) -> p h d", h=nh)
        qa3 = qa.rearrange("p (h d) -> p h d", h=nh)
        st = self.hn_stats[:, self.rr("hn", 2), :, :]
        hk = "hn_stats"
        QA = [KA, KA2]
        P.act(lambda h: h.copy(out=qf, in_=src), reads=[src_key], writes=[KF])
        P.dve(lambda h: h.tensor_tensor(out=qa, in0=qf, in1=qf, op=ALU.mult), reads=[KF], writes=QA)
        P.dve(lambda h: h.tensor_reduce(out=st[:, 0, 0:nh], in_=qa3, axis=AX.X, op=ALU.add), reads=QA, writes=[hk])
        P.act(lambda h: h.activation(out=st[:, 1, 0:nh], in_=st[:, 0, 0:nh], func=AF.Ln, scale=1.0 / HD, bias=EPS),
              reads=[hk], writes=[hk])
        P.act(lambda h: h.activation(out=st[:, 2, 0:nh], in_=st[:, 1, 0:nh], func=AF.Exp, scale=-0.5),
              reads=[hk], writes=[hk])
        P.dve(lambda h: h.tensor_tensor(out=qa3, in0=qf3, in1=st[:, 2, 0:nh].unsqueeze(2).to_broadcast([128, nh, HD]),
                                        op=ALU.mult),
              reads=[KF, hk], writes=QA)
        gb = gain[:].unsqueeze(1).to_broadcast([128, nh, HD])
        if ropet is None:
            P.dve(lambda h: h.tensor_tensor(out=dst, in0=qa3, in1=gb, op=ALU.mult),
                  reads=QA + [gain_key], writes=[dst_key])
            return
        P.dve(lambda h: h.tensor_tensor(out=qf3, in0=qa3, in1=gb, op=ALU.mult), reads=QA + [gain_key], writes=[KF])
        hh = HD // 2
        cosb = ropet[:, 0:hh].unsqueeze(1).to_broadcast([128, nh, hh])
        sinb = ropet[:, hh:HD].unsqueeze(1).to_broadcast([128, nh, hh])
        x1 = qf3[:, :, 0:hh]
        x2 = qf3[:, :, hh:HD]
        t1 = qa3[:, :, 0:hh]
        t2 = qa3[:, :, hh:HD]
        P.dve(lambda h: h.tensor_tensor(out=t1, in0=x1, in1=cosb, op=ALU.mult), reads=[KF, rope_key], writes=[KA])
        P.dve(lambda h: h.tensor_tensor(out=t2, in0=x2, in1=sinb, op=ALU.mult), reads=[KF, rope_key], writes=[KA2])
        P.dve(lambda h: h.tensor_tensor(out=dst[:, :, 0:hh], in0=t1, in1=t2, op=ALU.subtract),
              reads=QA, writes=[dst_key])
        P.dve(lambda h: h.tensor_tensor(out=t1, in0=x2, in1=cosb, op=ALU.mult), reads=[KF, rope_key], writes=[KA])
        P.dve(lambda h: h.tensor_tensor(out=t2, in0=x1, in1=sinb, op=ALU.mult), reads=[KF, rope_key], writes=[KA2])
        P.dve(lambda h: h.tensor_tensor(out=dst[:, :, hh:HD], in0=t1, in1=t2, op=ALU.add),
              reads=QA, writes=[dst_key])

    def emit_A_tile(self, kt, src_rows, mset, rope_rows, src_keys=()):
        P = self.P
        hi = self.rr("hTa", 2)
        hTa = self.hTa[hi]
        hk = ("hTa", hi)
        self.emit_hT(src_rows, 128, mset, hTa, hk, src_keys=src_keys, dma_eng="sp", pp="pa_t")
        ropet = None
        rk = None
        if rope_rows is not None:
            ri = self.rr("ropet", 2)
            ropet = self.ropet[ri]
            rk = ("ropet", ri)
            P.dma("sp", lambda h: h.dma_start(out=ropet[:], in_=rope_rows), writes=[rk])
        b_ = self.bank("pa_k")
        pk = self.psum[b_]
        wkv = self.wkv
        for k in range(8):
            P.pe(lambda h, k=k: h.matmul(pk[:], lhsT=hTa[:, k, :], rhs=wkv[:, k, :], start=(k == 0), stop=(k == 7)),
                 reads=[hk, self.wkv_key], writes=[("ps", b_)])
        V = self.V
        P.act(lambda h: h.copy(out=V[:, kt, :, 0:HD], in_=pk[:, 256:512].rearrange("p (h d) -> p h d", h=NKV)),
              reads=[("ps", b_)], writes=[("V", kt)])
        kri = self.rr("krot", 2)
        krot = self.krot[kri]
        krk = "krot%d" % kri
        kr3 = krot[:].rearrange("p (h d) -> p h d", h=NKV)
        self.emit_headnorm(pk[:, 0:256], ("ps", b_), NKV, self.gk, "gk", ropet, rk, kr3, krk)
        tb = self.bank("pa_k")
        ptb = self.psum[tb][:].bitcast(BF16)
        for pr in range(2):
            P.pe(lambda h, pr=pr: h.transpose(out=ptb[:, pr * 128:(pr + 1) * 128], in_=krot[:, pr * 128:(pr + 1) * 128],
                                              identity=self.identb[:]),
                 reads=[krk, "identb"], writes=[("ps", tb)])
        KT = self.KT
        P.dve(lambda h: h.tensor_copy(out=KT[:, :, kt * 128:(kt + 1) * 128],
                                      in_=ptb[:, 0:256].rearrange("p (a t) -> p a t", a=2)),
              reads=[("ps", tb)], writes=[("KT", kt)])

    def emit_B_hT(self, g, is_ctx, buf, pp="b"):
        P = self.P
        cur = self.cur
        mset = 1 if is_ctx else 0
        hT = self.hTs[buf]
        hkey = "hT%d" % buf
        x_src = cur["ctx_src"] if is_ctx else cur["x_src"]
        t0 = g * G
        ntl = G // 128
        steps = []
        for t in range(ntl):
            def s1(t=t):
                st2 = self.emit_hT(x_src[t0 + t * 128:t0 + (t + 1) * 128, :], 128, mset,
                                   hT[:, :, HALO + t * 128:HALO + (t + 1) * 128], hkey,
                                   src_keys=[("dram", x_src.name, t0 + t * 128)], pp=pp, two_stage=True)
                return st2
            steps.append(s1)
        if not is_ctx:
            lrow = (t0 - HALO) % SEQ
            rrow = (t0 + G) % SEQ

            def s1h():
                hhi = self.rr("hTa", 2)
                hh_ = self.hTa[hhi]
                hhk = ("hTa", hhi)
                ho = cur.get("halo_override", {})
                lsrc, lkey = x_src[lrow:lrow + HALO, :], ("dram", x_src.name, lrow // 128 * 128)
                rsrc, rkey = x_src[rrow:rrow + HALO, :], ("dram", x_src.name, rrow // 128 * 128)
                if (g, "L") in ho:
                    lsrc, lkey = ho[(g, "L")]
                if (g, "R") in ho:
                    rsrc, rkey = ho[(g, "R")]
                st2 = self.emit_hT(None, 47, mset, hh_[:, :, 0:47], hhk,
                                   row_dmas=[(0, HALO, lsrc), (32, HALO, rsrc)], src_keys=[lkey, rkey], pp=pp, two_stage=True)

                def s2h():
                    st2()
                    P.dve(lambda h: h.tensor_copy(out=hT[:, :, 0:HALO], in_=hh_[:, :, 0:HALO]), reads=[hhk], writes=[hkey])
                    P.dve(lambda h: h.tensor_copy(out=hT[:, :, HALO + G:GE], in_=hh_[:, :, 32:32 + HALO]), reads=[hhk], writes=[hkey])
                return s2h
            steps.append(s1h)
        else:
            def s1c():
                P.pool(lambda h: h.memset(hT[:, :, 0:HALO], 0.0), writes=[hkey])
                P.pool(lambda h: h.memset(hT[:, :, HALO + G:GE], 0.0), writes=[hkey])
                return lambda: None
            steps.append(s1c)
        return steps

    def emit_B_group(self, g, is_ctx, buf, next_steps=None):
        P = self.P
        cur = self.cur
        l = cur["l"]
        mset = 1 if is_ctx else 0
        hT = self.hTs[buf]
        HK = "hT%d" % buf
        x_src = cur["ctx_src"] if is_ctx else cur["x_src"]
        x_dst = cur["ctx_dst"] if is_ctx else cur["x_dst"]
        t0 = g * G
        ntl = G // 128
        lmask = rmask = None
        if not is_ctx:
            ng = SEQ // G
            if g == 0:
                lmask = 0
            elif g == ng // 2:
                lmask = 1
            if g == ng // 2 - 1:
                rmask = 1
            elif g == ng - 1:
                rmask = 0
        main = slice(HALO, HALO + G)
        wbin = self.wb_in[l]

        QT = self.QT
        wq = [self.load_w_block(wbin, O_Q + cb * 512) for cb in range(2)]
        def q_tile(t):
            if t % 2 == 1:
                for nm in ("qfbuf", "qnat", "hn"):
                    self.rr(nm, 2)
            qrot = self.qrot_t[t][:]
            qrk = "qrot%d" % t
            ropet = rk = None
            if not is_ctx:
                ri = self.rr("ropet", 2)
                ropet = self.ropet[ri]
                rk = ("ropet", ri)
                rows = self.rope[t0 + t * 128:t0 + (t + 1) * 128, :]
                P.dma("pool", lambda h, ropet=ropet, rows=rows: h.dma_start(out=ropet[:], in_=rows), writes=[rk])
            for cb in range(2):
                bq = self.bank("q4")
                pq = self.psum[bq]
                wqt, wqk = wq[cb]
                for k in range(8):
                    P.pe(lambda h, k=k, t=t, wqt=wqt, pq=pq: h.matmul(pq[:], lhsT=hT[:, k, HALO + t * 128:HALO + (t + 1) * 128],
                                                                      rhs=wqt[:, k, :], start=(k == 0), stop=(k == 7)),
                         reads=[wqk, HK], writes=[("ps", bq)])

                qni = self.rr("qnat", 2)
                qnat = self.qnat[qni]
                qnk = "qnat%d" % qni
                qn3 = qnat[:].rearrange("p (h d) -> p h d", h=8)
                self.emit_headnorm(pq[:], ("ps", bq), 8, self.gq, "gq", ropet, rk, qn3, qnk)
                P.dve(lambda h, cb=cb: h.tensor_copy(out=qrot[:, cb, :, :, :].rearrange("p j h d -> p h j d"),
                                                      in_=qnat[:].rearrange("p (h j d) -> p h j d", h=2, j=4)),
                       reads=[qnk], writes=[qrk])

        P.replay([P.capture(lambda t=t: q_tile(t)) for t in range(ntl)])

        def emit_q_transposes():
          for t in range(ntl):
            qrot = self.qrot_t[t][:]
            qrk = "qrot%d" % t
            tb = self.bank("b")
            ptb = self.psum[tb][:].bitcast(BF16)
            for cb in range(2):
                for j in range(4):
                    P.pe(lambda h, cb=cb, j=j: h.transpose(out=ptb[:, (cb * 4 + j) * 128:(cb * 4 + j + 1) * 128],
                                                           in_=qrot[:, cb, j, :, :].rearrange("p h d -> p (h d)"),
                                                           identity=self.identb[:]),
                         reads=[qrk, "identb"], writes=[("ps", tb)])
            ptv = ptb[:, 0:1024].rearrange("p (a j q) -> p a j q", a=2, j=4)
            P.dve(lambda h, t=t: h.tensor_copy(out=QT[0:64, 0::2, t, :, :], in_=ptv[0:64]),
                  reads=[("ps", tb)], writes=["QT"])
            P.dve(lambda h, t=t: h.tensor_copy(out=QT[64:128, 1::2, t, :, :], in_=ptv[64:128]),
                  reads=[("ps", tb)], writes=["QT"])
        P.transfer(self.attn_keys, self.conv_keys)
        yT = self.yT
        for blk in range(2):
            wg, wgk = self.load_w_block(wbin, O_UG + blk * 512)
            wa, wak = self.load_w_block(wbin, O_UA + blk * 512)
            for cc in range(4):
                c = blk * 4 + cc
                bg = self.bank("a")
                pg = self.psum[bg]
                for k in range(8):
                    P.pe(lambda h, k=k, cc=cc, wg=wg, pg=pg: h.matmul(pg[:, 0:GE], lhsT=wg[:, k, cc * 128:(cc + 1) * 128],
                                                                      rhs=hT[:, k, :], start=(k == 0), stop=(k == 7)),
                         reads=[wgk, HK], writes=[("ps", bg)])
                si = self.rr("sg", 2)
                sg = self.sg[si]
                P.act(lambda h, pg=pg, sg=sg: h.activation(out=sg, in_=pg[:, 0:GE], func=AF.Sigmoid),
                      reads=[("ps", bg)], writes=["sg%d" % si])
                ba = self.bank("b")
                pa = self.psum[ba]
                for k in range(8):
                    P.pe(lambda h, k=k, cc=cc, wa=wa, pa=pa: h.matmul(pa[:, 0:GE], lhsT=wa[:, k, cc * 128:(cc + 1) * 128],
                                                                      rhs=hT[:, k, :], start=(k == 0), stop=(k == 7)),
                         reads=[wak, HK], writes=[("ps", ba)])
                P.dve(lambda h, c=c, pa=pa, sg=sg: h.tensor_tensor(out=yT[:, c, :], in0=pa[:, 0:GE], in1=sg, op=ALU.mult),
                      reads=[("ps", ba), "sg%d" % si], writes=["yT"])
        emit_q_transposes()
        if is_ctx:
            P.pool(lambda h: h.memset(yT[:, :, 0:HALO], 0.0), reads=["yT"], writes=["yT"])
            P.pool(lambda h: h.memset(yT[:, :, HALO + G:GE], 0.0), reads=["yT"], writes=["yT"])
        else:
            if lmask is not None:
                P.dve(lambda h: h.tensor_scalar(out=yT[:, :, 0:HALO], in0=yT[:, :, 0:HALO],
                                                scalar1=self.mask_sb[:, lmask:lmask + 1], scalar2=None, op0=ALU.mult),
                      reads=["yT", "mask_sb"], writes=["yT"])
            if rmask is not None:
                P.dve(lambda h: h.tensor_scalar(out=yT[:, :, HALO + G:GE], in0=yT[:, :, HALO + G:GE],
                                                scalar1=self.mask_sb[:, rmask:rmask + 1], scalar2=None, op0=ALU.mult),
                      reads=["yT", "mask_sb"], writes=["yT"])
        ycb = self.ycb
        s1b, s2b = 6, 7
        ps1, ps2 = self.psum[s1b], self.psum[s2b]
        for c in range(8):
            dgi = c % 2
            diag = self.diag[dgi]
            dgk = "diag%d" % dgi
            P.pool(lambda h, c=c: h.tensor_tensor(out=diag[:],
                                                  in0=self.identb[:].unsqueeze(1).to_broadcast([128, CK, 128]),
                                                  in1=self.cw[:, c, 0:CK].unsqueeze(2).to_broadcast([128, CK, 128]),
                                                  op=ALU.mult),
                   reads=["identb", "cw"], writes=[dgk])
            bc = self.bank("a")
            pc = self.psum[bc]
            for j in range(CK):
                P.pe(lambda h, j=j, c=c, pc=pc: h.matmul(pc[:, 0:G], lhsT=diag[:, j, :], rhs=yT[:, c, j:j + G],
                                                         start=(j == 0), stop=(j == CK - 1)),
                     reads=[dgk, "yT"], writes=[("ps", bc)])
            P.act(lambda h, c=c, pc=pc: h.activation(out=ycb[:, c, :], in_=pc[:, 0:G], func=AF.Identity,
                                                     bias=self.pcol[:, c, 4:5], scale=1.0),
                  reads=[("ps", bc), "pcol"], writes=[("ycb", c)])
            qi = self.rr("sq", 2)
            sq = self.sq[qi]
            P.act(lambda h, c=c, pc=pc, sq=sq: h.activation(out=sq, in_=pc[:, 0:G], func=AF.Square,
                                                            bias=self.pcol[:, c, 4:5], scale=1.0),
                  reads=[("ps", bc), "pcol"], writes=["sq%d" % qi])
            P.pe(lambda h, c=c: h.matmul(ps1[:, 0:G], lhsT=self.onesb[:], rhs=ycb[:, c, :], start=(c == 0), stop=(c == 7)),
                 reads=["onesb", ("ycb", c)], writes=[("ps", s1b)])
            P.pe(lambda h, c=c, sq=sq: h.matmul(ps2[:, 0:G], lhsT=self.onesb[:], rhs=sq, start=(c == 0), stop=(c == 7)),
                 reads=["onesb", "sq%d" % qi], writes=[("ps", s2b)])
        mean, rstd, nmr = self.mean, self.rstd, self.nmr
        P.dve(lambda h: h.tensor_scalar(out=mean, in0=ps1[:, 0:G], scalar1=1.0 / D, scalar2=None, op0=ALU.mult),
              reads=[("ps", s1b)], writes=["mean"])
        P.dve(lambda h: h.tensor_tensor(out=nmr, in0=mean, in1=mean, op=ALU.mult), reads=["mean"], writes=["nmr"])
        P.dve(lambda h: h.scalar_tensor_tensor(out=rstd, in0=ps2[:, 0:G], scalar=1.0 / D, in1=nmr,
                                               op0=ALU.mult, op1=ALU.subtract),
              reads=[("ps", s2b), "nmr"], writes=["rstd"])
        P.act(lambda h: h.activation(out=rstd, in_=rstd, func=AF.Ln, scale=1.0, bias=EPS), reads=["rstd"], writes=["rstd"])
        P.act(lambda h: h.activation(out=rstd, in_=rstd, func=AF.Exp, scale=-0.5), reads=["rstd"], writes=["rstd"])
        P.dve(lambda h: h.scalar_tensor_tensor(out=nmr, in0=mean, scalar=-1.0, in1=rstd, op0=ALU.mult, op1=ALU.mult),
              reads=["mean", "rstd"], writes=["nmr"])
        uT = self.uT
        for c in range(8):
            zi = self.rr("ztmp", 2)
            z = self.ztmp[zi]
            zk = "ztmp%d" % zi
            P.dve(lambda h, c=c, z=z: h.tensor_tensor(out=z, in0=ycb[:, c, :], in1=rstd, op=ALU.mult),
                  reads=[("ycb", c), "rstd"], writes=[zk])
            P.dve(lambda h, z=z: h.tensor_tensor(out=z, in0=z, in1=nmr, op=ALU.add), reads=[zk, "nmr"], writes=[zk])
            P.act(lambda h, c=c, z=z: h.activation(out=uT[:, c, :], in_=z, func=AF.Silu, scale=self.pcol[:, c, 5:6],
                                                   bias=self.pcol[:, c, 6:7]),
                  reads=[zk, "pcol"], writes=[("uT", c)])
        for blk in range(2):
            wga, wgak = self.load_w_block(wbin, O_GA + blk * 512)
            for cc in range(4):
                c = blk * 4 + cc
                bb = self.bank("a")
                pp = self.psum[bb]
                for k in range(8):
                    P.pe(lambda h, k=k, cc=cc, wga=wga, pp=pp: h.matmul(pp[:, 0:G], lhsT=wga[:, k, cc * 128:(cc + 1) * 128],
                                                                        rhs=hT[:, k, main], start=(k == 0), stop=(k == 7)),
                         reads=[wgak, HK], writes=[("ps", bb)])
                gi = self.rr("gat", 2)
                gat = self.gat[gi]
                P.act(lambda h, pp=pp, gat=gat: h.activation(out=gat, in_=pp[:, 0:G], func=AF.Silu),
                      reads=[("ps", bb)], writes=["gat%d" % gi])
                P.dve(lambda h, c=c, gat=gat: h.tensor_tensor(out=uT[:, c, :], in0=uT[:, c, :], in1=gat, op=ALU.mult),
                       reads=[("uT", c), "gat%d" % gi], writes=[("uT", c)])
        m1 = self.m1
        for blk in range(2):
            wco, wcok = self.load_w_block(self.wb_co[l], blk * 512)
            wma, wmak = self.load_w_block(wbin, O_MA + blk * 512)
            for cc in range(4):
                oc = blk * 4 + cc
                bm_ = self.bank("a")
                pmg = self.psum[bm_]
                for k in range(8):
                    P.pe(lambda h, k=k, cc=cc, wma=wma, pmg=pmg: h.matmul(pmg[:, 0:G], lhsT=wma[:, k, cc * 128:(cc + 1) * 128],
                                                                          rhs=hT[:, k, main], start=(k == 0), stop=(k == 7)),
                         reads=[wmak, HK], writes=[("ps", bm_)])
                zi = self.rr("ztmp", 2)
                z = self.ztmp[zi]
                zk = "ztmp%d" % zi
                P.act(lambda h, pmg=pmg, z=z: h.activation(out=z, in_=pmg[:, 0:G], func=AF.Sigmoid),
                      reads=[("ps", bm_)], writes=[zk])
                by = self.bank("b")
                py = self.psum[by]
                for c in range(8):
                    P.pe(lambda h, c=c, cc=cc, wco=wco, py=py: h.matmul(py[:, 0:G], lhsT=wco[:, c, cc * 128:(cc + 1) * 128],
                                                                        rhs=uT[:, c, :], start=(c == 0), stop=(c == 7)),
                         reads=[wcok, ("uT", c)], writes=[("ps", by)])
                P.dve(lambda h, oc=oc, py=py, z=z: h.tensor_tensor(out=m1[:, oc, :], in0=py[:, 0:G], in1=z, op=ALU.mult),
                      reads=[("ps", by), zk], writes=[("m1", oc)])

        P.transfer(self.conv_keys, self.attn_keys)
        QT = self.QT
        keytiles = [64, 65] if is_ctx else list(range(NKT))
        og = self.og
        nkt = len(keytiles)
        iters = [(hkv, t, ii, kt) for hkv in range(NKV) for t in range(ntl) for ii, kt in enumerate(keytiles)]
        SB = [1, 3, 4, 5]
        OB = [6, 2]
        LOOK = 3
        st_ = dict(gb={}, wgb={})
        st_["wgb"][0] = self.load_w_block(wbin, O_GB, 256)

        def emit_qk(n):
            hkv, t, ii, kt = iters[n]
            kvp, half = hkv // 2, hkv % 2
            pl = slice(64 * half, 64 * half + 64)
            bs = SB[n % 4]
            pS = self.psum[bs]
            qv = QT[:, hkv, t, :, :].rearrange("p j q -> p (j q)")
            P.pe(lambda h: h.matmul(pS[:], lhsT=self.KT[:, kvp, kt * 128:(kt + 1) * 128], rhs=qv, start=True, stop=True),
                 reads=[("KT", kt), "QT"], writes=[("ps", bs)])

        def emit_gate_b(hkv):
            gi = hkv % 2
            gbT = self.gbT[gi]
            gk_ = "gbT%d" % gi
            if hkv not in st_["wgb"]:
                st_["wgb"][hkv] = self.load_w_block(wbin, O_GB + hkv * 256, 256)
            wgb, wgbk = st_["wgb"].pop(hkv)
            if hkv + 1 < NKV:
                st_["wgb"][hkv + 1] = self.load_w_block(wbin, O_GB + (hkv + 1) * 256, 256)
            for j in range(4):
                bgb = self.bank("a1")
                pgb = self.psum[bgb]
                for k in range(8):
                    P.pe(lambda h, k=k: h.matmul(pgb[0:64, 0:G], lhsT=wgb[:, k, j * 64:(j + 1) * 64],
                                                 rhs=hT[:, k, main], start=(k == 0), stop=(k == 7)),
                         reads=[wgbk, HK], writes=[("ps", bgb)])
                P.act(lambda h: h.activation(out=gbT[0:64, j, :], in_=pgb[0:64, 0:G], func=AF.Silu),
                      reads=[("ps", bgb)], writes=[gk_])
            st_["gb"][hkv] = (gbT, gk_)

        def emit_epi1(blk):
            hkv, t = blk
            bo = OB[(hkv * ntl + t) % 2]
            po = self.psum[bo]
            oi = self.rr("oaug", 2)
            oaug = self.oaug[oi]
            ok_ = "oaug%d" % oi
            P.dve(lambda h: h.tensor_copy(out=oaug[0:HD + 1, :], in_=po[0:HD + 1, :]),
                  reads=[("ps", bo)], writes=[ok_])
            return (hkv, t, oaug, ok_)

        def emit_epi2(e):
            hkv, t, oaug, ok_ = e
            gbT, gk_ = st_["gb"][hkv]
            bb_ = 7
            pbc = self.psum[bb_]
            P.pe(lambda h: h.matmul(pbc[:], lhsT=self.sel[0:HD + 1, :], rhs=oaug[0:HD + 1, :], start=True, stop=True),
                 reads=["sel", ok_], writes=[("ps", bb_)])
            rbc = self.rbc
            P.dve(lambda h: h.reciprocal(out=rbc[0:64, :], in_=pbc[0:64, :]), reads=[("ps", bb_)], writes=["rbc"])
            P.dve(lambda h: h.tensor_tensor(out=oaug[0:64, :], in0=oaug[0:64, :], in1=rbc[0:64, :], op=ALU.mult),
                  reads=[ok_, "rbc"], writes=[ok_])
            P.dve(lambda h: h.tensor_tensor(out=og[0:64, 4 * hkv:4 * hkv + 4, t * 128:(t + 1) * 128],
                                            in0=oaug[0:64, :].rearrange("p (j q) -> p j q", j=4),
                                            in1=gbT[0:64, :, t * 128:(t + 1) * 128], op=ALU.mult),
                  reads=[ok_, gk_], writes=["og"])

        pending = []
        N = len(iters)
        sched = {}
        if next_steps:
            gap = max(1, min(24, (N - 8) // (2 * len(next_steps) + 1)))
            for si_, stp in enumerate(next_steps):
                sched[4 + 2 * si_ * gap] = ("s1", si_)
                sched[4 + (2 * si_ + 1) * gap] = ("s2", si_)
        st2s = {}
        for n in range(min(LOOK, N)):
            emit_qk(n)
        for n in range(N):
            hkv, t, ii, kt = iters[n]
            if t == 0 and ii == min(8, nkt - 1):
                emit_gate_b(hkv)
            bs = SB[n % 4]
            pS = self.psum[bs]
            pi = self.rr("PT", 4)
            PT = self.PT[pi]
            P.act(lambda h: h.activation(out=PT, in_=pS[:], func=AF.Exp, scale=SCALE),
                  reads=[("ps", bs)], writes=["PT%d" % pi])
            bo = OB[(hkv * ntl + t) % 2]
            po = self.psum[bo]
            vo = (kt * NKV + hkv) * (HD + 1)
            P.pe(lambda h: h.matmul(po[:, :], lhsT=self.Vflat[:, vo:vo + 128], rhs=PT,
                                    start=(ii == 0), stop=(ii == nkt - 1)),
                 reads=[("V", kt), "Vones", "PT%d" % pi], writes=[("ps", bo)])
            if n + LOOK < N:
                emit_qk(n + LOOK)
            while pending and pending[0][0] <= n:
                emit_epi2(pending.pop(0)[1])
            if ii == nkt - 1:
                pending.append((n + 3, emit_epi1((hkv, t))))
            if n in sched:
                kind, si_ = sched[n]
                if kind == "s1":
                    st2s[si_] = next_steps[si_]()
                else:
                    st2s.pop(si_)()
        while pending:
            emit_epi2(pending.pop(0)[1])
        if next_steps:
            for si_ in range(len(next_steps)):
                if si_ in st2s:
                    st2s.pop(si_)()
                elif not any(v == ("s1", si_) and k_ < N for k_, v in sched.items()):
                    next_steps[si_]()()
        wao_src = self.wb_ao[l]
        for blk in range(4):
            i, wt, wk = self.wslot()
            wv = wt[0:64, :, :].rearrange("p k c -> p (k c)").rearrange("p (h c) -> p h c", h=16)
            src = wao_src[:, blk * 256:(blk + 1) * 256].rearrange("(h p) c -> p h c", p=64)
            P.dma("sp", lambda h, wv=wv, src=src: h.dma_start(out=wv, in_=src), reads=[("wb", wao_src.name, (blk * 256) // 512)], writes=[wk])
            if blk % 2 == 0:
                wmb, wmbk = self.load_w_block(wbin, O_MB + (blk // 2) * 512)
            for o2 in range(2):
                oc = blk * 2 + o2
                cc = oc % 4
                bm_ = self.bank("a")
                pmg = self.psum[bm_]
                for k in range(8):
                    P.pe(lambda h, k=k, cc=cc, wmb=wmb, pmg=pmg: h.matmul(pmg[:, 0:G], lhsT=wmb[:, k, cc * 128:(cc + 1) * 128],
                                                                          rhs=hT[:, k, main], start=(k == 0), stop=(k == 7)),
                         reads=[wmbk, HK], writes=[("ps", bm_)])
                si = self.rr("sgb", 2)
                sgb = self.sgb[si]
                sk_ = "sgb%d" % si
                P.act(lambda h, pmg=pmg, sgb=sgb: h.activation(out=sgb, in_=pmg[:, 0:G], func=AF.Sigmoid),
                      reads=[("ps", bm_)], writes=[sk_])
                by = self.bank("b")
                py = self.psum[by]
                for hd in range(NH):
                    P.pe(lambda h, hd=hd, o2=o2, wv=wv, py=py: h.matmul(py[:, 0:G], lhsT=wv[:, hd, o2 * 128:(o2 + 1) * 128],
                                                                        rhs=og[0:64, hd, :], start=(hd == 0), stop=(hd == NH - 1)),
                         reads=[wk, "og"], writes=[("ps", by)])
                P.dve(lambda h, py=py, sgb=sgb: h.tensor_tensor(out=sgb, in0=py[:, 0:G], in1=sgb, op=ALU.mult),
                      reads=[("ps", by), sk_], writes=[sk_])
                P.dve(lambda h, oc=oc, sgb=sgb: h.tensor_tensor(out=m1[:, oc, :], in0=m1[:, oc, :], in1=sgb, op=ALU.add),
                       reads=[("m1", oc), sk_], writes=[("m1", oc)])
        wo = [self.load_w_block(self.wb_o[l], cb * 512) for cb in range(2)]
        def out_tile(t):
            if t % 2 == 1:
                self.rr("otmp", 2)
            xi = self.rr("xt", 2)
            xr = self.xt[xi]
            xk = "xt%d" % xi
            rows = x_src[t0 + t * 128:t0 + (t + 1) * 128, :]
            P.dma("pool", lambda h, xr=xr, rows=rows: h.dma_start(out=xr[:], in_=rows),
                  reads=[("dram", x_src.name, t0 + t * 128)], writes=[xk])
            si = self.rr("small", 8)
            sk = ("small", si)
            sm = self.small[:, si * 4:si * 4 + 4]
            pbs = []
            for cb in range(2):
                b_ = self.bank("q4")
                po_ = self.psum[b_]
                pbs.append((b_, po_))
                wot, wok = wo[cb]
                for c in range(8):
                    P.pe(lambda h, c=c, t=t, wot=wot, po_=po_: h.matmul(po_[:], lhsT=m1[:, c, t * 128:(t + 1) * 128],
                                                                        rhs=wot[:, c, :], start=(c == 0), stop=(c == 7)),
                         reads=[wok, ("m1", c)], writes=[("ps", b_)])
                P.act(lambda h, cb=cb, po_=po_, sm=sm: h.activation(out=self.junk[:, 0:512], in_=po_[:], func=AF.Square,
                                                                    accum_out=sm[:, cb:cb + 1]),
                      reads=[("ps", b_)], writes=["junk", sk])
            P.dve(lambda h, sm=sm: h.tensor_tensor(out=sm[:, 2:3], in0=sm[:, 0:1], in1=sm[:, 1:2], op=ALU.add),
                  reads=[sk], writes=[sk])
            P.act(lambda h, sm=sm: h.activation(out=sm[:, 3:4], in_=sm[:, 2:3], func=AF.Ln, scale=1.0 / D, bias=EPS),
                  reads=[sk], writes=[sk])
            P.act(lambda h, sm=sm: h.activation(out=sm[:, 2:3], in_=sm[:, 3:4], func=AF.Exp, scale=-0.5),
                  reads=[sk], writes=[sk])
            for cb in range(2):
                b_, po_ = pbs[cb]
                oi = self.rr("otmp", 2)
                ot = self.otmp[oi][:]
                otk = "otmp%d" % oi
                P.dve(lambda h, cb=cb, po_=po_, ot=ot: h.tensor_tensor(out=ot, in0=po_[:],
                                                                       in1=self.gtg[:, mset, cb * 512:(cb + 1) * 512], op=ALU.mult),
                      reads=[("ps", b_), "gtg"], writes=[otk])
                P.dve(lambda h, cb=cb, ot=ot, xr=xr, sm=sm: h.scalar_tensor_tensor(out=xr[:, cb * 512:(cb + 1) * 512], in0=ot,
                                                                                   scalar=sm[:, 2:3],
                                                                                   in1=xr[:, cb * 512:(cb + 1) * 512],
                                                                                   op0=ALU.mult, op1=ALU.add),
                      reads=[otk, sk, xk], writes=[xk])
            drow = t0 + t * 128
            if (not is_ctx) and x_dst.shape[0] < SEQ and drow >= x_dst.shape[0]:
                return
            dst = x_dst[drow:drow + 128, :]
            is_final = (l == self.layers[-1])
            idx = P.dma("pool", lambda h, xr=xr, dst=dst: h.dma_start(out=dst, in_=xr[:]), reads=[xk],
                        writes=[("dram", x_dst.name, drow)])

        P.replay([P.capture(lambda t=t: out_tile(t)) for t in range(ntl)])


import os
DBG_GROUPS = os.environ.get("KDBG_GROUPS")


def _make_builder(layers):
    ng = SEQ // G
    b_groups = {}
    a_tiles = {}
    for l in layers:
        lastl = (l == DEPTH - 1)
        if USE_CC:
            b_groups[l] = list(range(ng // 2))
            a_tiles[l] = list(range(HALF // 128))
        else:
            b_groups[l] = list(range(ng // 2)) if lastl else list(range(ng))
            a_tiles[l] = list(range(SEQ // 128))
    if DBG_GROUPS:
        for l in layers:
            b_groups[l] = [int(v) for v in DBG_GROUPS.split(",") if v != ""]
    ctx_upd = [l for l in layers if l != DEPTH - 1]
    out_rows = HALF if layers[-1] == DEPTH - 1 else SEQ
    return Builder(layers, b_groups, a_tiles, ctx_upd, out_rows)


def _rope_tables():
    n = SEQ
    row = np.repeat(np.arange(n // 64, dtype=np.float32), 64)
    col = np.tile(np.arange(64, dtype=np.float32), n // 64)
    inv = (10000.0 ** (-np.arange(0, 32, 2, dtype=np.float32) / 32.0)).astype(np.float32)
    ang = np.concatenate([row[:, None] * inv, col[:, None] * inv], axis=-1).astype(np.float32)
    return np.concatenate([np.cos(ang), np.sin(ang)], axis=-1).astype(np.float32)


def _perm_rows(a, half):
    if half == 0:
        return np.ascontiguousarray(a)
    return np.ascontiguousarray(np.concatenate([a[HALF:], a[:HALF]], axis=0))


LAUNCH_LAYERS = [[0, 1]]
USE_CC = True
RG_PAIRS = [[0, 1], [2, 3], [4, 5], [6, 7]]


def kernel(x, c, ctx, c_ctx, w_mod, b_mod, g_pre, g_post, w_in, conv_w, conv_b,
           ln_g, ln_b, w_conv_out, q_norm_g, k_norm_g, w_attn_out, w_out):
    f = lambda a: np.ascontiguousarray(np.asarray(a, dtype=np.float32))
    x, c, ctx, c_ctx = f(x), f(c), f(ctx), f(c_ctx)
    shared = dict(w_mod=f(w_mod), b_mod=f(b_mod), g_pre=f(g_pre), g_post=f(g_post), w_in=f(w_in),
                  conv_w=f(conv_w), conv_b=f(conv_b), ln_g=f(ln_g), ln_b=f(ln_b), w_conv_out=f(w_conv_out),
                  q_norm_g=f(q_norm_g), k_norm_g=f(k_norm_g), w_attn_out=f(w_attn_out), w_out=f(w_out))
    rope = _rope_tables()
    xs = []
    cs = []
    for core in range(8):
        b, half = core // 2, core % 2
        xs.append(_perm_rows(x[b], half))
        cs.append(ctx[b])
    for layers in LAUNCH_LAYERS:
        bld = _make_builder(layers)
        bld.build()
        in_maps = []
        for core in range(8):
            b, half = core // 2, core % 2
            m = dict(shared)
            m["x"] = xs[core]
            m["ctx"] = cs[core]
            m["cvec"] = np.ascontiguousarray(np.stack([c[b], c_ctx], axis=0))
            m["rope"] = _perm_rows(rope, half)
            mk = np.zeros((128, 2), np.float32)
            mk[:, 0] = 1.0 if half == 1 else 0.0
            mk[:, 1] = 1.0 if half == 0 else 0.0
            m["masks"] = mk
            in_maps.append(m)
        res = run_bass_kernel_spmd(bld.nc, in_maps, core_ids=list(range(8)))
        outs = [r["out"] for r in res.results]
        if layers[-1] != DEPTH - 1:
            xs = [np.asarray(o, dtype=np.float32) for o in outs]
            cs = [np.asarray(r["ctx_out"], dtype=np.float32) for r in res.results]
    out = np.zeros((NB, SEQ, D), np.float32)
    for core in range(8):
        b, half = core // 2, core % 2
        out[b, half * HALF:(half + 1) * HALF] = np.asarray(outs[core], dtype=np.float32)
    return out
```

```python
import contextlib
import types
import numpy as np
import concourse.bass as bass
import concourse.mybir as mybir
from concourse.bass_utils import run_bass_kernel_spmd

F32 = mybir.dt.float32
BF16 = mybir.dt.bfloat16
AF = mybir.ActivationFunctionType
ALU = mybir.AluOpType
AX = mybir.AxisListType

D = 1024
NB = 4
SEQ = 8192
HALF = 4096
CTX = 256
DEPTH = 2
NH = 16
NKV = 4
HD = 64
G = 256
HALO = 15
GE = G + 2 * HALO
NKT = 66
EPS = 1e-6
SCALE = 0.125
CK = 31
INW = 7680
O_UA, O_UG, O_GA, O_Q, O_K, O_V, O_GB, O_MA, O_MB = 0, 1024, 2048, 3072, 4096, 4352, 4608, 5632, 6656

ENGS = ("pe", "act", "dve", "pool", "sp")
NDMA = 8


class _Op:
    __slots__ = ("eng", "fn", "deps", "dma", "signaled", "tok", "prev_same_sem", "inc", "writes")

    def __init__(self, eng, fn, deps, dma, inc, writes=()):
        self.writes = tuple(writes)
        self.eng = eng
        self.fn = fn
        self.deps = deps
        self.dma = dma
        self.signaled = dma
        self.tok = None
        self.prev_same_sem = None
        self.inc = inc


class Prog:
    def __init__(self, nc, same_engine_sync=True):
        self.nc = nc
        self.ops = []
        self.last_w = {}
        self.readers = {}
        self.pending = {}
        self.same_engine_sync = same_engine_sync

    def transfer(self, old_keys, new_keys):
        dd = set()
        for k in old_keys:
            w = self.last_w.get(k)
            if w is not None:
                dd.add(w)
            rd = self.readers.get(k)
            if rd:
                dd.update(rd[0].values())
                dd.update(rd[1])
            if k in self.pending:
                dd.update(self.pending[k])
        for k in new_keys:
            self.pending.setdefault(k, set()).update(dd)

    @staticmethod
    def _freeze(fn):
        if not fn.__closure__:
            return fn
        cells = []
        for c in fn.__closure__:
            try:
                cells.append(types.CellType(c.cell_contents))
            except ValueError:
                cells.append(c)
        f2 = types.FunctionType(fn.__code__, fn.__globals__, fn.__name__, fn.__defaults__, tuple(cells))
        f2.__kwdefaults__ = fn.__kwdefaults__
        return f2

    def capture(self, f):
        self._capture = []
        f()
        ops, self._capture = self._capture, None
        return ops

    def replay(self, lists):
        n = max(len(x) for x in lists)
        for i in range(n):
            for x in lists:
                if i < len(x):
                    self.op(*x[i])

    def op(self, eng, fn, reads=(), writes=(), dma=False, inc=None):
        fn = self._freeze(fn)
        if getattr(self, "_capture", None) is not None:
            self._capture.append((eng, fn, tuple(reads), tuple(writes), dma, inc))
            return None
        idx = len(self.ops)
        deps = set()
        for r in reads:
            w = self.last_w.get(r)
            if w is not None:
                deps.add(w)
            if r in self.pending:
                deps.update(self.pending[r])
        for w_ in writes:
            w = self.last_w.get(w_)
            if w is not None:
                deps.add(w)
            rd = self.readers.get(w_)
            if rd:
                deps.update(rd[0].values())
                deps.update(rd[1])
            if w_ in self.pending:
                deps.update(self.pending.pop(w_))
        for r in reads:
            rd = self.readers.setdefault(r, ({}, []))
            if dma:
                rd[1].append(idx)
            else:
                rd[0][eng] = idx
        for w_ in writes:
            self.last_w[w_] = idx
            self.readers[w_] = ({}, [])
        deps.discard(idx)
        self.ops.append(_Op(eng, fn, deps, dma, inc if inc is not None else (16 if dma else 1), writes))
        return idx

    def pe(self, fn, reads=(), writes=()):
        return self.op("pe", fn, reads, writes)

    def act(self, fn, reads=(), writes=()):
        return self.op("act", fn, reads, writes)

    def dve(self, fn, reads=(), writes=()):
        return self.op("dve", fn, reads, writes)

    def pool(self, fn, reads=(), writes=()):
        return self.op("pool", fn, reads, writes)

    def dma(self, eng, fn, reads=(), writes=()):
        return self.op(eng, fn, reads, writes, dma=True)

    def _skip(self, od, o):
        return (not od.dma) and (not o.dma) and od.eng == o.eng and (od.eng == "pe" or not self.same_engine_sync)

    def emit(self, final_wait_ops=()):
        nc = self.nc
        ops = self.ops
        for o in ops:
            for d in o.deps:
                od = ops[d]
                if od.dma or self._skip(od, o):
                    continue
                od.signaled = True
        for d in final_wait_ops:
            ops[d].signaled = True
        with contextlib.ExitStack() as st:
            esem = {e: st.enter_context(nc.semaphore("s_" + e)) for e in ENGS}
            dsem = {e: [st.enter_context(nc.semaphore("d_%s%d" % (e, i))) for i in range(NDMA)]
                    for e in ("sp", "pool", "act")}
            ecount = {e: 0 for e in ENGS}
            dcount = {e: 0 for e in dsem}
            dlast = {e: [None] * NDMA for e in dsem}
            dval = {e: [0] * NDMA for e in dsem}
            ccsem = st.enter_context(nc.semaphore("s_cc"))
            cccount = 0
            for i, o in enumerate(ops):
                if o.dma and o.inc == 1:
                    cccount += 1
                    o.tok = (ccsem, cccount, ("cc",))
                elif o.dma:
                    n = dcount[o.eng]
                    dcount[o.eng] += 1
                    slot = n % NDMA
                    dval[o.eng][slot] += o.inc
                    o.tok = (dsem[o.eng][slot], dval[o.eng][slot], ("d", o.eng, slot))
                    o.prev_same_sem = dlast[o.eng][slot]
                    dlast[o.eng][slot] = i
                elif o.signaled:
                    ecount[o.eng] += 1
                    o.tok = (esem[o.eng], ecount[o.eng], ("e", o.eng))
            block = st.enter_context(nc.Block())
            handles = {"pe": nc.tensor, "act": nc.scalar, "dve": nc.vector,
                       "pool": nc.gpsimd, "sp": nc.sync}

            def stream(e):
                h = handles[e]
                waited = {}

                def wait_tok(tok):
                    sem, val, key = tok
                    if waited.get(key, 0) >= val:
                        return
                    waited[key] = val
                    h.wait_ge(sem, val)

                for i, o in enumerate(ops):
                    if o.eng != e:
                        continue
                    for d in sorted(o.deps):
                        od = ops[d]
                        if od.tok is None or self._skip(od, o):
                            continue
                        wait_tok(od.tok)
                    if o.dma and o.prev_same_sem is not None:
                        wait_tok(ops[o.prev_same_sem].tok)
                    ins = o.fn(h)
                    if o.dma and o.inc == 1:
                        ins.then_inc(o.tok[0])
                    elif o.dma:
                        ins.then_inc(o.tok[0], o.inc)
                    elif o.signaled:
                        ins.then_inc(o.tok[0], 1)
                if e == "sp":
                    for d in final_wait_ops:
                        wait_tok(ops[d].tok)

            block.tensor(lambda h: stream("pe"))
            block.scalar(lambda h: stream("act"))
            block.vector(lambda h: stream("dve"))
            block.gpsimd(lambda h: stream("pool"))
            block.sync(lambda h: stream("sp"))
        return ecount, dcount


class Builder:
    def __init__(self, layers, b_groups, a_tiles, ctx_update_layers, out_rows):
        self.layers = layers
        self.b_groups = b_groups
        self.a_tiles = a_tiles
        self.ctx_update_layers = ctx_update_layers
        self.out_rows = out_rows
        self.nc = bass.Bass("TRN2", target_bir_lowering=False)
        self.P = Prog(self.nc)
        self.final_ops = []
        self._rr = {}

    def dram_in(self, name, shape, dt=F32):
        return self.nc.dram_tensor(name, list(shape), dt, kind="ExternalInput").ap()

    def dram_out(self, name, shape, dt=F32):
        return self.nc.dram_tensor(name, list(shape), dt, kind="ExternalOutput").ap()

    def dram_tmp(self, name, shape, dt=F32):
        return self.nc.dram_tensor(name, list(shape), dt, kind="Internal").ap()

    def sb(self, name, shape, dt):
        return self.st.enter_context(self.nc.sbuf_tensor(name, list(shape), dt))

    def rr(self, pool, n):
        i = self._rr.get(pool, 0)
        self._rr[pool] = i + 1
        return i % n

    def bank(self, pool):
        banks = self.pools[pool]
        return banks[self.rr(("bank", pool), len(banks))]

    def build(self):
        nc = self.nc
        P = self.P
        L = DEPTH
        self.x_in = self.dram_in("x", [SEQ, D])
        self.ctx_in = self.dram_in("ctx", [CTX, D])
        self.cvec = self.dram_in("cvec", [2, D])
        self.w_mod = self.dram_in("w_mod", [L, D, 3 * D])
        self.b_mod = self.dram_in("b_mod", [L, 3 * D])
        self.g_pre = self.dram_in("g_pre", [L, D])
        self.g_post = self.dram_in("g_post", [L, D])
        self.w_in = self.dram_in("w_in", [L, D, INW])
        self.conv_w = self.dram_in("conv_w", [L, CK, D])
        self.conv_b = self.dram_in("conv_b", [L, D])
        self.ln_g = self.dram_in("ln_g", [L, D])
        self.ln_b = self.dram_in("ln_b", [L, D])
        self.w_co = self.dram_in("w_conv_out", [L, D, D])
        self.q_g = self.dram_in("q_norm_g", [L, HD])
        self.k_g = self.dram_in("k_norm_g", [L, HD])
        self.w_ao = self.dram_in("w_attn_out", [L, D, D])
        self.w_o = self.dram_in("w_out", [L, D, D])
        self.rope = self.dram_in("rope", [SEQ, HD])
        self.masks = self.dram_in("masks", [128, 2])
        self.out = self.dram_out("out", [self.out_rows, D])
        last = self.layers[-1]
        self.ctx_out = None
        if last in self.ctx_update_layers:
            self.ctx_out = self.dram_out("ctx_out", [CTX, D])
        self.x_mid = None
        if len(self.layers) > 1:
            self.x_mid = self.dram_tmp("x_mid", [SEQ, D])
            self.ctx_mid = self.dram_tmp("ctx_mid", [CTX, D])
        if USE_CC:
            self.cc_in_k = {l: self.dram_tmp("cc_in_k%d" % l, [256, HALF], BF16) for l in self.layers}
            self.cc_out_k = {l: self.dram_tmp("cc_out_k%d" % l, [512, HALF], BF16) for l in self.layers}
            self.cc_in_v = {(l, j): self.dram_tmp("cc_in_v%d_%d" % (l, j), [128 * 16, 288], BF16)
                            for l in self.layers for j in range(2)}
            self.cc_out_v = {(l, j): self.dram_tmp("cc_out_v%d_%d" % (l, j), [2 * 128 * 16, 288], BF16)
                             for l in self.layers for j in range(2)}
            self.cc_in_h = self.dram_tmp("cc_in_h", [32, D], F32)
            self.cc_out_h = self.dram_tmp("cc_out_h", [64, D], F32)
        self.wb_in = {l: self.dram_tmp("wb_in%d" % l, [INW // 512, 128, 4096], BF16) for l in self.layers}
        self.wb_gb = {l: self.dram_tmp("wb_gb%d" % l, [4, 128, 2048], BF16) for l in self.layers}
        self.wb_co = {l: self.dram_tmp("wb_co%d" % l, [2, 128, 4096], BF16) for l in self.layers}
        self.wb_ao = {l: self.dram_tmp("wb_ao%d" % l, [4, 64, 4096], BF16) for l in self.layers}
        self.wb_o = {l: self.dram_tmp("wb_o%d" % l, [2, 128, 4096], BF16) for l in self.layers}

        with contextlib.ExitStack() as st:
            self.st = st
            sb = self.sb
            self.KT = sb("KT", [128, 2, NKT * 128], BF16)
            self.Vflat = sb("V", [128, NKT * NKV * (HD + 1) + 64], BF16)
            self.V = self.Vflat[:, 0:NKT * NKV * (HD + 1)].rearrange("p (t h d) -> p t h d", t=NKT, h=NKV)
            self.identf = sb("identf", [128, 128], F32)
            self.identb = sb("identb", [128, 128], BF16)
            self.onesf = sb("onesf", [128, 128], F32)
            self.onesb = sb("onesb", [128, 128], BF16)
            self.sel = sb("sel", [128, 128], F32)
            self.mask_sb = sb("mask_sb", [128, 2], F32)
            self.pcol = sb("pcol", [128, 8, 16], F32)
            self.cw = sb("cw", [128, 8, 32], F32)
            self.csl = sb("csl", [128, 8, 2], F32)
            self.modc = sb("modc", [128, 3, 8, 2], F32)
            self.AT = sb("AT", [128, 8, 2], F32)
            self.gcol = sb("gcol", [128, 8, 2], F32)
            self.gtg = sb("gtg", [128, 2, D], F32)
            self.dg = sb("dg", [128, 2, 128], F32)
            self.gq = sb("gq", [128, HD], F32)
            self.gk = sb("gk", [128, HD], F32)
            self.NW = 4
            self.wsl = [sb("wsl%d" % i, [128, 8, 512], BF16) for i in range(self.NW)]
            self.xt = [sb("xt%d" % i, [128, D], F32) for i in range(2)]
            self.junk = sb("junk", [128, D], BF16)
            self.small = sb("small", [128, 64], F32)
            self.ropet = [sb("ropet%d" % i, [128, HD], F32) for i in range(2)]
            self.hTa = [sb("hTa%d" % i, [128, 8, 128], BF16) for i in range(2)]
            self.hTs = [sb("hT%d" % i, [128, 8, GE], BF16) for i in range(2)]
            self.m1 = sb("m1", [128, 8, G], BF16)
            self.qf = [sb("qf%d" % i, [128, 512], F32) for i in range(2)]
            self.qa = [sb("qa%d" % i, [128, 512], F32) for i in range(2)]
            self.krot = [sb("krot%d" % i, [128, 256], BF16) for i in range(2)]
            self.hn_stats = sb("hn_stats", [128, 2, 3, 8], F32)
            self.zpad = sb("zpad", [128, 16, 28], BF16)
            self.QT_t = sb("QT", [128, 4, 2, 4, 128], BF16)
            self.QT = self.QT_t[:]
            self.qrot_t = [sb("qrot%d" % i, [128, 2, 4, 2, 64], BF16) for i in range(2)]
            self.qnat = [sb("qnat%d" % i, [128, 512], BF16) for i in range(2)]
            self.otmp = [sb("otmp%d" % i, [128, 512], F32) for i in range(2)]
            UB = 40 * 1024
            self.U = sb("U", [128, UB // 2], BF16)
            self.carve()
            self.psum = [st.enter_context(nc.psum_tensor("ps%d" % i, [128, 512], F32)) for i in range(8)]
            self.pools = {"a": [0, 1], "b": [2, 3], "s": [4, 5], "o": [6], "m": [7], "pa_t": [0, 1, 2, 3], "pa_k": [4, 5, 6, 7], "q4": [0, 1, 2, 3], "a1": [0]}

            self.emit_consts()
            self.emit_casts(self.layers[0])
            for l in self.layers:
                self.emit_layer(l)
            fin_names = [self.out.name] + ([self.ctx_out.name] if self.ctx_out is not None else [])
            self.final_ops = [i for i, o in enumerate(P.ops)
                              if any(isinstance(w, tuple) and len(w) == 3 and w[0] == "dram" and w[1] in fin_names
                                     for w in o.writes)]
            counts = P.emit(final_wait_ops=self.final_ops)
        return counts

    def carve(self):
        U = self.U
        off = [0]

        def take(nbytes):
            o = off[0]
            off[0] += (nbytes + 63) // 64 * 64 // 2
            return o

        def vb(nel):
            o = take(nel * 2)
            return U[:, o:o + nel]

        def vf(nel):
            o = take(nel * 4)
            return U[:, o:o + 2 * nel].bitcast(F32)

        self.yT = vb(8 * GE).rearrange("p (c t) -> p c t", c=8)
        self.sg = [vf(GE), vf(GE)]
        self.ycb = vb(8 * G).rearrange("p (c t) -> p c t", c=8)
        self.sq = [vb(G), vb(G)]
        self.mean = vf(G)
        self.rstd = vf(G)
        self.nmr = vf(G)
        self.ztmp = [vf(G), vf(G)]
        self.uT = vb(8 * G).rearrange("p (c t) -> p c t", c=8)
        self.gat = [vb(G), vb(G)]
        self.diag = [vb(CK * 128).rearrange("p (j m) -> p j m", j=CK) for _ in range(2)]
        conv_end = off[0]
        off[0] = 0
        self.og = vb(16 * G).rearrange("p (h t) -> p h t", h=16)
        self.gbT = [vb(4 * G).rearrange("p (j t) -> p j t", j=4) for _ in range(2)]
        self.PT = [vb(512) for _ in range(4)]
        self.oaug = [vf(512) for _ in range(2)]
        self.rbc = vf(512)
        self.sgb = [vf(G), vf(G)]
        attn_end = off[0]
        assert max(conv_end, attn_end) <= self.U.shape[1], (conv_end, attn_end, self.U.shape)
        self.conv_keys = ["yT", "sg0", "sg1", "ycb", "sq0", "sq1", "mean", "rstd", "nmr", "ztmp0", "ztmp1",
                          "uT", "gat0", "gat1", "diag0", "diag1"]
        self.attn_keys = ["og", "gbT0", "gbT1", "PT0", "PT1", "PT2", "PT3", "oaug0", "oaug1", "rbc", "sgb0", "sgb1"]

    def emit_consts(self):
        P = self.P
        identf, identb, onesf, onesb, sel = self.identf, self.identb, self.onesf, self.onesb, self.sel
        P.pool(lambda h: h.memset(identf[:], 1.0), writes=["identf"])
        P.pool(lambda h: h.affine_select(out=identf[:], in_=identf[:], pattern=[[-1, 128]],
                                          compare_op=ALU.is_equal, fill=0.0, base=0, channel_multiplier=1),
               reads=["identf"], writes=["identf"])
        P.dve(lambda h: h.tensor_copy(out=identb[:], in_=identf[:]), reads=["identf"], writes=["identb"])
        P.pool(lambda h: h.memset(onesf[:], 1.0), writes=["onesf"])
        P.pool(lambda h: h.memset(onesb[:], 1.0), writes=["onesb"])
        P.pool(lambda h: h.memset(sel[:], 1.0), writes=["sel"])
        P.pool(lambda h: h.affine_select(out=sel[:], in_=sel[:], pattern=[[0, 128]],
                                          compare_op=ALU.is_equal, fill=0.0, base=-64, channel_multiplier=1),
               reads=["sel"], writes=["sel"])
        V = self.V
        P.pool(lambda h: h.memset(V[:, :, :, HD:HD + 1], 1.0), writes=["Vones"])
        P.pool(lambda h: h.memset(self.Vflat[:, NKT * NKV * (HD + 1):], 0.0), writes=["Vpad"])
        P.dma("sp", lambda h: h.dma_start(out=self.mask_sb[:], in_=self.masks), writes=["mask_sb"])
        P.pool(lambda h: h.memset(self.zpad[:], 0.0), writes=["zpad"])
        if USE_CC:
            for (l_, j_), civ in self.cc_in_v.items():
                P.dma("sp", lambda h, civ=civ: h.dma_start(out=civ.rearrange("(p t) c -> p t c", t=16)[:, :, 260:288],
                                                           in_=self.zpad[:]),
                      reads=["zpad"], writes=[("ccpad", civ.name)])
        QT = self.QT
        P.pool(lambda h: h.memset(QT[64:128, 0::2, :, :, :], 0.0), writes=["QT"])
        P.pool(lambda h: h.memset(QT[0:64, 1::2, :, :, :], 0.0), writes=["QT"])

    def wslot(self):
        i = self.rr("wslot", self.NW)
        return i, self.wsl[i], ("wsl", i)

    def load_w_block(self, src2d, col0, ncols=512):
        i, t, key = self.wslot()
        tf = t[:].rearrange("p k c -> p (k c)")
        l = self.cur["l"]
        if src2d is self.wb_in[l] and O_GB <= col0 < O_MA:
            assert ncols == 256
            u = (col0 - O_GB) // 256
            src = self.wb_gb[l][u]
            rk = ("wb", self.wb_gb[l].name, u)
            view = tf[:, 0:2048].rearrange("p (k c) -> p k c", k=8)
            self.P.dma("sp", lambda h: h.dma_start(out=tf[:, 0:2048], in_=src), reads=[rk], writes=[key])
            return view, key
        assert ncols == 512 and col0 % 512 == 0
        src = src2d[col0 // 512]
        self.P.dma("sp", lambda h: h.dma_start(out=tf, in_=src), reads=[("wb", src2d.name, col0 // 512)], writes=[key])
        return t, key

    def emit_casts(self, l, deferred=False):
        P = self.P
        wi, wbi = self.w_in[l], self.wb_in[l]
        order = [(wi, wbi, O_K), (wi, wbi, O_Q), (wi, wbi, O_Q + 512),
                 (wi, wbi, O_UG), (wi, wbi, O_UA), (wi, wbi, O_UG + 512), (wi, wbi, O_UA + 512),
                 (wi, wbi, O_GA), (wi, wbi, O_GA + 512),
                 (self.w_co[l], self.wb_co[l], 0), (wi, wbi, O_MA), (self.w_co[l], self.wb_co[l], 512), (wi, wbi, O_MA + 512),
                 (wi, wbi, O_GB), (wi, wbi, O_GB + 512),
                 (self.w_ao[l], self.wb_ao[l], 0), (wi, wbi, O_MB), (self.w_ao[l], self.wb_ao[l], 512), (wi, wbi, O_MB + 512),
                 (self.w_o[l], self.wb_o[l], 0), (self.w_o[l], self.wb_o[l], 512)]
        todo = []
        wgbt = self.wb_gb[l]
        wao, waot = self.w_ao[l], self.wb_ao[l]
        for src, dst, c0 in order:
            if dst is wbi and O_GB <= c0 < O_MA:
                for u in range(2):
                    uu = (c0 - O_GB) // 256 + u
                    todo.append(lambda uu=uu: P.dma(
                        "pool", lambda h: h.dma_start(out=wgbt[uu].rearrange("p (k c) -> p k c", k=8),
                                                      in_=wi[:, O_GB + uu * 256:O_GB + (uu + 1) * 256].rearrange("(k p) c -> p k c", p=128)),
                        writes=[("wb", wgbt.name, uu)]))
            elif dst is waot:
                for u in range(2):
                    uu = c0 // 256 + u
                    todo.append(lambda uu=uu: P.dma(
                        "pool", lambda h: h.dma_start(out=waot[uu].rearrange("p (h c) -> p h c", h=16),
                                                      in_=wao[:, uu * 256:(uu + 1) * 256].rearrange("(h p) c -> p h c", p=64)),
                        writes=[("wb", waot.name, uu)]))
            else:
                todo.append(lambda src=src, dst=dst, c0=c0: P.dma(
                    "pool", lambda h: h.dma_start(out=dst[c0 // 512].rearrange("p (k c) -> p k c", k=8),
                                                  in_=src[:, c0:c0 + 512].rearrange("(k p) c -> p k c", p=128)),
                    writes=[("wb", dst.name, c0 // 512)]))
        if deferred:
            return todo
        for f in todo:
            f()

    def emit_layer(self, l):
        P = self.P
        nc = self.nc
        first = (l == self.layers[0])
        lastl = (l == self.layers[-1])
        x_src = self.x_in if first else self.x_mid
        ctx_src = self.ctx_in if first else self.ctx_mid
        if lastl:
            x_dst, ctx_dst = self.out, self.ctx_out
        else:
            x_dst, ctx_dst = self.x_mid, self.ctx_mid
        self.cur = dict(l=l, x_src=x_src, ctx_src=ctx_src, x_dst=x_dst, ctx_dst=ctx_dst)

        prow = self.xt[1]
        rows = [self.cvec[0:1, :], self.cvec[1:2, :], self.g_pre[l:l + 1, :], self.g_post[l:l + 1, :],
                self.conv_b[l:l + 1, :], self.ln_g[l:l + 1, :], self.ln_b[l:l + 1, :],
                self.b_mod[l:l + 1, 0:D], self.b_mod[l:l + 1, D:2 * D], self.b_mod[l:l + 1, 2 * D:3 * D]]
        for r, src in enumerate(rows):
            P.dma("sp", lambda h, r=r, src=src: h.dma_start(out=prow[r:r + 1, :], in_=src), writes=["xt1"])
        cwrow = self.xt[0]
        P.dma("sp", lambda h: h.dma_start(out=cwrow[0:CK, :], in_=self.conv_w[l]), writes=["xt0"])
        pb = self.bank("m")
        ps = self.psum[pb]
        for k in range(8):
            P.pe(lambda h, k=k: h.transpose(out=ps[:, k * 16:k * 16 + 10], in_=prow[0:10, k * 128:(k + 1) * 128],
                                            identity=self.identf[0:10, 0:10]),
                 reads=["xt1", "identf"], writes=[("ps", pb)])
        pcol = self.pcol
        P.dve(lambda h: h.tensor_copy(out=pcol[:, :, 0:10], in_=ps[:, 0:128].rearrange("p (k r) -> p k r", k=8)[:, :, 0:10]),
              reads=[("ps", pb)], writes=["pcol"])
        pb2 = self.bank("a")
        ps2 = self.psum[pb2]
        for k in range(8):
            P.pe(lambda h, k=k: h.transpose(out=ps2[:, k * 32:k * 32 + CK], in_=cwrow[0:CK, k * 128:(k + 1) * 128],
                                            identity=self.identf[0:CK, 0:CK]),
                 reads=["xt0", "identf"], writes=[("ps", pb2)])
        cw = self.cw
        P.dve(lambda h: h.tensor_copy(out=cw[:, :, 0:CK], in_=ps2[:, 0:256].rearrange("p (k r) -> p k r", k=8)[:, :, 0:CK]),
              reads=[("ps", pb2)], writes=["cw"])
        csl = self.csl
        P.act(lambda h: h.activation(out=csl[:], in_=pcol[:, :, 0:2], func=AF.Silu), reads=["pcol"], writes=["csl"])
        pm_b = self.bank("m")
        pm = self.psum[pm_b]
        for blk in range(12):
            i, t, key = self.wslot()
            tf = t[:].rearrange("p k c -> p (k c)").bitcast(F32).rearrange("p (k c) -> p k c", k=8)
            src = self.w_mod[l][:, blk * 256:(blk + 1) * 256].rearrange("(k p) c -> p k c", p=128)
            P.dma("sp", lambda h, tf=tf, src=src: h.dma_start(out=tf, in_=src), writes=[key])
            for f2 in range(2):
                fc = blk * 2 + f2
                for k in range(8):
                    P.pe(lambda h, tf=tf, f2=f2, k=k, fc=fc: h.matmul(pm[:, fc * 2:fc * 2 + 2],
                                                                      lhsT=tf[:, k, f2 * 128:(f2 + 1) * 128],
                                                                      rhs=csl[:, k, :], start=(k == 0), stop=(k == 7)),
                         reads=[key, "csl"], writes=[("ps", pm_b)])
        modc = self.modc
        bm = pcol[:, :, 7:10].rearrange("p f r -> p r f").unsqueeze(3).to_broadcast([128, 3, 8, 2])
        P.dve(lambda h: h.tensor_tensor(out=modc[:], in0=pm[:, 0:48].rearrange("p (r f s) -> p r f s", r=3, f=8),
                                        in1=bm, op=ALU.add),
              reads=[("ps", pm_b), "pcol"], writes=["modc"])
        AT, gcol = self.AT, self.gcol
        P.dve(lambda h: h.tensor_scalar(out=AT[:], in0=modc[:, 1, :, :], scalar1=1.0, scalar2=None, op0=ALU.add),
              reads=["modc"], writes=["AT"])
        P.dve(lambda h: h.tensor_tensor(out=AT[:], in0=AT[:], in1=pcol[:, :, 2:3].to_broadcast([128, 8, 2]), op=ALU.mult),
              reads=["AT", "pcol"], writes=["AT"])
        P.dve(lambda h: h.tensor_tensor(out=gcol[:], in0=modc[:, 2, :, :], in1=pcol[:, :, 3:4].to_broadcast([128, 8, 2]),
                                        op=ALU.mult),
              reads=["modc", "pcol"], writes=["gcol"])
        for s in range(2):
            for half in range(2):
                gb_ = self.bank("b")
                pg = self.psum[gb_]
                for kk in range(4):
                    k = half * 4 + kk
                    di = self.rr("dg", 2)
                    dgt = self.dg[:, di, :]
                    P.dve(lambda h, dgt=dgt, k=k, s=s: h.tensor_scalar(out=dgt, in0=self.identf[:], scalar1=gcol[:, k, s:s + 1],
                                                                       scalar2=None, op0=ALU.mult),
                          reads=["identf", "gcol"], writes=[("dg", di)])
                    P.pe(lambda h, dgt=dgt, kk=kk: h.matmul(pg[:, kk * 128:(kk + 1) * 128], lhsT=self.onesf[:], rhs=dgt,
                                                            start=True, stop=True),
                         reads=[("dg", di), "onesf"], writes=[("ps", gb_)])
                P.act(lambda h, s=s, half=half, pg=pg: h.copy(out=self.gtg[:, s, half * 512:(half + 1) * 512], in_=pg[:]),
                      reads=[("ps", gb_)], writes=["gtg"])
        P.dma("sp", lambda h: h.dma_start(out=self.gq[:], in_=self.q_g[l].partition_broadcast(128)), writes=["gq"])
        P.dma("sp", lambda h: h.dma_start(out=self.gk[:], in_=self.k_g[l].partition_broadcast(128)), writes=["gk"])

        self.wkv, self.wkv_key = self.load_w_block(self.wb_in[l], O_K, 512)
        jobs = []
        for kt in self.a_tiles[l]:
            jobs.append(lambda kt=kt: self.emit_A_tile(kt, x_src[kt * 128:(kt + 1) * 128, :], 0,
                                                       self.rope[kt * 128:(kt + 1) * 128, :],
                                                       src_keys=[("dram", x_src.name, kt * 128)]))
        for ci in range(2):
            jobs.append(lambda ci=ci: self.emit_A_tile(64 + ci, ctx_src[ci * 128:(ci + 1) * 128, :], 1, None,
                                                       src_keys=[("dram", ctx_src.name, ci * 128)]))
        for j0 in range(0, len(jobs), 2):
            P.replay([P.capture(j) for j in jobs[j0:j0 + 2]])
        if USE_CC:
            self.emit_kv_exchange(l)
        self.cur["halo_override"] = {}
        if USE_CC and not first:
            ng2 = len(self.b_groups[l])
            self.cur["halo_override"] = {(0, "L"): (self.cc_out_h[17:32, :], "cc_out_h"),
                                         (ng2 - 1, "R"): (self.cc_out_h[32:47, :], "cc_out_h")}
        items = [(g, False) for g in self.b_groups[l]]
        if l in self.ctx_update_layers:
            items.append((0, True))
        for st1 in self.emit_B_hT(items[0][0], items[0][1], 0):
            st1()()
        cast_todo = [] if lastl else self.emit_casts(self.layers[self.layers.index(l) + 1], deferred=True)
        for i_, (g, isc) in enumerate(items):
            nxt = None
            if i_ + 1 < len(items):
                nxt = self.emit_B_hT(items[i_ + 1][0], items[i_ + 1][1], (i_ + 1) % 2, pp="a1")
            self.emit_B_group(g, isc, i_ % 2, nxt)
            ncast = len(cast_todo) if i_ == len(items) - 1 else min(2, len(cast_todo))
            for _ in range(ncast):
                cast_todo.pop(0)()
        if USE_CC and not lastl:
            self.emit_halo_exchange()

    def emit_kv_exchange(self, l):
        P = self.P
        KT, V = self.KT, self.V
        nt = HALF // 128
        VW = NKV * (HD + 1)
        cik, cok = self.cc_in_k[l], self.cc_out_k[l]
        kkeys = [("KT", kt) for kt in range(nt)]
        P.dma("sp", lambda h: h.dma_start(out=cik.rearrange("(p a) k -> p a k", a=2), in_=KT[:, :, 0:HALF]),
              reads=kkeys, writes=[("cc", cik.name)])
        for j in range(2):
            civ = self.cc_in_v[(l, j)]
            P.dma("sp", lambda h: h.dma_start(out=civ.rearrange("(p t) c -> p t c", t=16)[:, :, 0:VW],
                                              in_=V[:, j * 16:(j + 1) * 16, :, :].rearrange("p t h d -> p t (h d)")),
                  reads=[("V", j * 16 + kt) for kt in range(16)] + ["Vones"], writes=[("cc", civ.name)])
        P.op("pool", lambda h: h.collective_compute("AllGather", ALU.bypass, replica_groups=RG_PAIRS, ins=[cik], outs=[cok]),
             reads=[("cc", cik.name)], writes=[("cc", cok.name)], dma=True, inc=1)
        for j in range(2):
            civ, cov = self.cc_in_v[(l, j)], self.cc_out_v[(l, j)]
            P.op("pool", lambda h: h.collective_compute("AllGather", ALU.bypass, replica_groups=RG_PAIRS, ins=[civ], outs=[cov]),
                 reads=[("cc", civ.name), ("ccpad", civ.name)], writes=[("cc", cov.name)], dma=True, inc=1)
        for r in range(2):
            P.dma("sp", lambda h: h.dma_start(out=KT[:, :, r * HALF:(r + 1) * HALF],
                                              in_=cok[r * 256:(r + 1) * 256, :].rearrange("(p a) k -> p a k", a=2)),
                  reads=[("cc", cok.name)], writes=[("KT", r * nt + kt) for kt in range(nt)])
            for j in range(2):
                cov = self.cc_out_v[(l, j)]
                t0_ = r * nt + j * 16
                P.dma("sp", lambda h: h.dma_start(out=V[:, t0_:t0_ + 16, :, :].rearrange("p t h d -> p t (h d)"),
                                                  in_=cov[r * 2048:(r + 1) * 2048, :].rearrange("(p t) c -> p t c", t=16)[:, :, 0:VW]),
                      reads=[("cc", cov.name)], writes=[("V", t0_ + kt) for kt in range(16)])

    def emit_halo_exchange(self):
        P = self.P
        xm = self.x_mid
        cih, coh = self.cc_in_h, self.cc_out_h
        P.dma("sp", lambda h: h.dma_start(out=cih[0:16, :], in_=xm[0:16, :]),
              reads=[("dram", xm.name, 0)], writes=["cc_in_h"])
        P.dma("sp", lambda h: h.dma_start(out=cih[16:32, :], in_=xm[HALF - 16:HALF, :]),
              reads=[("dram", xm.name, HALF - 128)], writes=["cc_in_h"])
        P.op("pool", lambda h: h.collective_compute("AllGather", ALU.bypass, replica_groups=RG_PAIRS, ins=[cih], outs=[coh]),
             reads=["cc_in_h"], writes=["cc_out_h"], dma=True, inc=1)

    def emit_hT(self, src_rows, nrows, mset, dst, dst_key, row_dmas=None, src_keys=(), dma_eng="pool", pp="b", two_stage=False):
        P = self.P
        xi = self.rr("xt", 2)
        xt = self.xt[xi]
        xk = "xt%d" % xi
        if row_dmas is None:
            P.dma(dma_eng, lambda h: h.dma_start(out=xt[0:nrows, :], in_=src_rows), reads=list(src_keys), writes=[xk])
        else:
            P.pool(lambda h: h.memset(xt[0:nrows, :], 0.0), writes=[xk])
            for (r0, n, src) in row_dmas:
                P.dma("pool", lambda h, r0=r0, n=n, src=src: h.dma_start(out=xt[r0:r0 + n, :], in_=src), reads=list(src_keys), writes=[xk])
        si = self.rr("small", 8)
        ss = self.small[:, si * 4:si * 4 + 1]
        ln = self.small[:, si * 4 + 1:si * 4 + 2]
        rs = self.small[:, si * 4 + 2:si * 4 + 3]
        sk = ("small", si)
        junk = self.junk
        P.dve(lambda h: h.scalar_tensor_tensor(out=junk[0:nrows, :], in0=xt[0:nrows, :], scalar=1.0, in1=xt[0:nrows, :],
                                               op0=ALU.mult, op1=ALU.mult, accum_out=ss[0:nrows, :]),
              reads=[xk], writes=["junk", sk])
        P.act(lambda h: h.activation(out=ln[0:nrows, :], in_=ss[0:nrows, :], func=AF.Ln, scale=1.0 / D, bias=EPS),
              reads=[sk], writes=[sk])
        P.act(lambda h: h.activation(out=rs[0:nrows, :], in_=ln[0:nrows, :], func=AF.Exp, scale=-0.5),
              reads=[sk], writes=[sk])
        P.dve(lambda h: h.tensor_scalar(out=xt[0:nrows, :], in0=xt[0:nrows, :], scalar1=rs[0:nrows, :], scalar2=None,
                                        op0=ALU.mult),
              reads=[xk, sk], writes=[xk])
        if two_stage:
            return lambda: self.emit_hT_stage2(xt, xk, nrows, mset, dst, dst_key, pp)
        self.emit_hT_stage2(xt, xk, nrows, mset, dst, dst_key, pp)

    def emit_hT_stage2(self, xt, xk, nrows, mset, dst, dst_key, pp):
        P = self.P
        for half in range(2):
            b_ = self.bank(pp)
            pt = self.psum[b_]
            for kk in range(4):
                k = half * 4 + kk
                P.pe(lambda h, k=k, kk=kk, pt=pt: h.transpose(out=pt[:, kk * 128:kk * 128 + nrows],
                                                              in_=xt[0:nrows, k * 128:(k + 1) * 128],
                                                              identity=self.identf[0:nrows, 0:nrows]),
                     reads=[xk, "identf"], writes=[("ps", b_)])
            for kk in range(4):
                k = half * 4 + kk
                if kk % 2 == 0 and pp != "a1":
                    P.act(lambda h, k=k, kk=kk, pt=pt: h.activation(out=dst[:, k, :], in_=pt[:, kk * 128:kk * 128 + nrows],
                                                                    func=AF.Identity, scale=self.AT[:, k, mset:mset + 1],
                                                                    bias=self.modc[:, 0, k, mset:mset + 1]),
                          reads=[("ps", b_), "AT", "modc"], writes=[dst_key])
                else:
                    P.dve(lambda h, k=k, kk=kk, pt=pt: h.tensor_scalar(out=dst[:, k, :], in0=pt[:, kk * 128:kk * 128 + nrows],
                                                                       scalar1=self.AT[:, k, mset:mset + 1],
                                                                       scalar2=self.modc[:, 0, k, mset:mset + 1],
                                                                       op0=ALU.mult, op1=ALU.add),
                          reads=[("ps", b_), "AT", "modc"], writes=[dst_key])

    def emit_headnorm(self, src, src_key, nh, gain, gain_key, ropet, rope_key, dst, dst_key):
        P = self.P
        n = nh * HD
        bi = self.rr("qfbuf", 2)
        qf = self.qf[bi][:, 0:n]
        qa = self.qa[bi][:, 0:n]
        KF, KA, KA2 = "qf%d" % bi, "qa%d" % bi, "qa2_%d" % bi
        qf3 = qf.rearrange("p (h d) -> p h d", h=nh)
        qa3 = qa.rearrange("p (h d) -> p h d", h=nh)
        st = self.hn_stats[:, self.rr("hn", 2), :, :]
        hk = "hn_stats"
        QA = [KA, KA2]
        P.act(lambda h: h.copy(out=qf, in_=src), reads=[src_key], writes=[KF])
        P.dve(lambda h: h.tensor_tensor(out=qa, in0=qf, in1=qf, op=ALU.mult), reads=[KF], writes=QA)
        P.dve(lambda h: h.tensor_reduce(out=st[:, 0, 0:nh], in_=qa3, axis=AX.X, op=ALU.add), reads=QA, writes=[hk])
        P.act(lambda h: h.activation(out=st[:, 1, 0:nh], in_=st[:, 0, 0:nh], func=AF.Ln, scale=1.0 / HD, bias=EPS),
              reads=[hk], writes=[hk])
        P.act(lambda h: h.activation(out=st[:, 2, 0:nh], in_=st[:, 1, 0:nh], func=AF.Exp, scale=-0.5),
              reads=[hk], writes=[hk])
        P.dve(lambda h: h.tensor_tensor(out=qa3, in0=qf3, in1=st[:, 2, 0:nh].unsqueeze(2).to_broadcast([128, nh, HD]),
                                        op=ALU.mult),
              reads=[KF, hk], writes=QA)
        gb = gain[:].unsqueeze(1).to_broadcast([128, nh, HD])
        if ropet is None:
            P.dve(lambda h: h.tensor_tensor(out=dst, in0=qa3, in1=gb, op=ALU.mult),
                  reads=QA + [gain_key], writes=[dst_key])
            return
        P.dve(lambda h: h.tensor_tensor(out=qf3, in0=qa3, in1=gb, op=ALU.mult), reads=QA + [gain_key], writes=[KF])
        hh = HD // 2
        cosb = ropet[:, 0:hh].unsqueeze(1).to_broadcast([128, nh, hh])
        sinb = ropet[:, hh:HD].unsqueeze(1).to_broadcast([128, nh, hh])
        x1 = qf3[:, :, 0:hh]
        x2 = qf3[:, :, hh:HD]
        t1 = qa3[:, :, 0:hh]
        t2 = qa3[:, :, hh:HD]
        P.dve(lambda h: h.tensor_tensor(out=t1, in0=x1, in1=cosb, op=ALU.mult), reads=[KF, rope_key], writes=[KA])
        P.dve(lambda h: h.tensor_tensor(out=t2, in0=x2, in1=sinb, op=ALU.mult), reads=[KF, rope_key], writes=[KA2])
        P.dve(lambda h: h.tensor_tensor(out=dst[:, :, 0:hh], in0=t1, in1=t2, op=ALU.subtract),
              reads=QA, writes=[dst_key])
        P.dve(lambda h: h.tensor_tensor(out=t1, in0=x2, in1=cosb, op=ALU.mult), reads=[KF, rope_key], writes=[KA])
        P.dve(lambda h: h.tensor_tensor(out=t2, in0=x1, in1=sinb, op=ALU.mult), reads=[KF, rope_key], writes=[KA2])
        P.dve(lambda h: h.tensor_tensor(out=dst[:, :, hh:HD], in0=t1, in1=t2, op=ALU.add),
              reads=QA, writes=[dst_key])

    def emit_A_tile(self, kt, src_rows, mset, rope_rows, src_keys=()):
        P = self.P
        hi = self.rr("hTa", 2)
        hTa = self.hTa[hi]
        hk = ("hTa", hi)
        self.emit_hT(src_rows, 128, mset, hTa, hk, src_keys=src_keys, dma_eng="sp", pp="pa_t")
        ropet = None
        rk = None
        if rope_rows is not None:
            ri = self.rr("ropet", 2)
            ropet = self.ropet[ri]
            rk = ("ropet", ri)
            P.dma("sp", lambda h: h.dma_start(out=ropet[:], in_=rope_rows), writes=[rk])
        b_ = self.bank("pa_k")
        pk = self.psum[b_]
        wkv = self.wkv
        for k in range(8):
            P.pe(lambda h, k=k: h.matmul(pk[:], lhsT=hTa[:, k, :], rhs=wkv[:, k, :], start=(k == 0), stop=(k == 7)),
                 reads=[hk, self.wkv_key], writes=[("ps", b_)])
        V = self.V
        P.act(lambda h: h.copy(out=V[:, kt, :, 0:HD], in_=pk[:, 256:512].rearrange("p (h d) -> p h d", h=NKV)),
              reads=[("ps", b_)], writes=[("V", kt)])
        kri = self.rr("krot", 2)
        krot = self.krot[kri]
        krk = "krot%d" % kri
        kr3 = krot[:].rearrange("p (h d) -> p h d", h=NKV)
        self.emit_headnorm(pk[:, 0:256], ("ps", b_), NKV, self.gk, "gk", ropet, rk, kr3, krk)
        tb = self.bank("pa_k")
        ptb = self.psum[tb][:].bitcast(BF16)
        for pr in range(2):
            P.pe(lambda h, pr=pr: h.transpose(out=ptb[:, pr * 128:(pr + 1) * 128], in_=krot[:, pr * 128:(pr + 1) * 128],
                                              identity=self.identb[:]),
                 reads=[krk, "identb"], writes=[("ps", tb)])
        KT = self.KT
        P.dve(lambda h: h.tensor_copy(out=KT[:, :, kt * 128:(kt + 1) * 128],
                                      in_=ptb[:, 0:256].rearrange("p (a t) -> p a t", a=2)),
              reads=[("ps", tb)], writes=[("KT", kt)])

    def emit_B_hT(self, g, is_ctx, buf, pp="b"):
        P = self.P
        cur = self.cur
        mset = 1 if is_ctx else 0
        hT = self.hTs[buf]
        hkey = "hT%d" % buf
        x_src = cur["ctx_src"] if is_ctx else cur["x_src"]
        t0 = g * G
        ntl = G // 128
        steps = []
        for t in range(ntl):
            def s1(t=t):
                st2 = self.emit_hT(x_src[t0 + t * 128:t0 + (t + 1) * 128, :], 128, mset,
                                   hT[:, :, HALO + t * 128:HALO + (t + 1) * 128], hkey,
                                   src_keys=[("dram", x_src.name, t0 + t * 128)], pp=pp, two_stage=True)
                return st2
            steps.append(s1)
        if not is_ctx:
            lrow = (t0 - HALO) % SEQ
            rrow = (t0 + G) % SEQ

            def s1h():
                hhi = self.rr("hTa", 2)
                hh_ = self.hTa[hhi]
                hhk = ("hTa", hhi)
                ho = cur.get("halo_override", {})
                lsrc, lkey = x_src[lrow:lrow + HALO, :], ("dram", x_src.name, lrow // 128 * 128)
                rsrc, rkey = x_src[rrow:rrow + HALO, :], ("dram", x_src.name, rrow // 128 * 128)
                if (g, "L") in ho:
                    lsrc, lkey = ho[(g, "L")]
                if (g, "R") in ho:
                    rsrc, rkey = ho[(g, "R")]
                st2 = self.emit_hT(None, 47, mset, hh_[:, :, 0:47], hhk,
                                   row_dmas=[(0, HALO, lsrc), (32, HALO, rsrc)], src_keys=[lkey, rkey], pp=pp, two_stage=True)

                def s2h():
                    st2()
                    P.dve(lambda h: h.tensor_copy(out=hT[:, :, 0:HALO], in_=hh_[:, :, 0:HALO]), reads=[hhk], writes=[hkey])
                    P.dve(lambda h: h.tensor_copy(out=hT[:, :, HALO + G:GE], in_=hh_[:, :, 32:32 + HALO]), reads=[hhk], writes=[hkey])
                return s2h
            steps.append(s1h)
        else:
            def s1c():
                P.pool(lambda h: h.memset(hT[:, :, 0:HALO], 0.0), writes=[hkey])
                P.pool(lambda h: h.memset(hT[:, :, HALO + G:GE], 0.0), writes=[hkey])
                return lambda: None
            steps.append(s1c)
        return steps

    def emit_B_group(self, g, is_ctx, buf, next_steps=None):
        P = self.P
        cur = self.cur
        l = cur["l"]
        mset = 1 if is_ctx else 0
        hT = self.hTs[buf]
        HK = "hT%d" % buf
        x_src = cur["ctx_src"] if is_ctx else cur["x_src"]
        x_dst = cur["ctx_dst"] if is_ctx else cur["x_dst"]
        t0 = g * G
        ntl = G // 128
        lmask = rmask = None
        if not is_ctx:
            ng = SEQ // G
            if g == 0:
                lmask = 0
            elif g == ng // 2:
                lmask = 1
            if g == ng // 2 - 1:
                rmask = 1
            elif g == ng - 1:
                rmask = 0
        main = slice(HALO, HALO + G)
        wbin = self.wb_in[l]

        QT = self.QT
        wq = [self.load_w_block(wbin, O_Q + cb * 512) for cb in range(2)]
        def q_tile(t):
            if t % 2 == 1:
                for nm in ("qfbuf", "qnat", "hn"):
                    self.rr(nm, 2)
            qrot = self.qrot_t[t][:]
            qrk = "qrot%d" % t
            ropet = rk = None
            if not is_ctx:
                ri = self.rr("ropet", 2)
                ropet = self.ropet[ri]
                rk = ("ropet", ri)
                rows = self.rope[t0 + t * 128:t0 + (t + 1) * 128, :]
                P.dma("pool", lambda h, ropet=ropet, rows=rows: h.dma_start(out=ropet[:], in_=rows), writes=[rk])
            for cb in range(2):
                bq = self.bank("q4")
                pq = self.psum[bq]
                wqt, wqk = wq[cb]
                for k in range(8):
                    P.pe(lambda h, k=k, t=t, wqt=wqt, pq=pq: h.matmul(pq[:], lhsT=hT[:, k, HALO + t * 128:HALO + (t + 1) * 128],
                                                                      rhs=wqt[:, k, :], start=(k == 0), stop=(k == 7)),
                         reads=[wqk, HK], writes=[("ps", bq)])

                qni = self.rr("qnat", 2)
                qnat = self.qnat[qni]
                qnk = "qnat%d" % qni
                qn3 = qnat[:].rearrange("p (h d) -> p h d", h=8)
                self.emit_headnorm(pq[:], ("ps", bq), 8, self.gq, "gq", ropet, rk, qn3, qnk)
                P.dve(lambda h, cb=cb: h.tensor_copy(out=qrot[:, cb, :, :, :].rearrange("p j h d -> p h j d"),
                                                      in_=qnat[:].rearrange("p (h j d) -> p h j d", h=2, j=4)),
                       reads=[qnk], writes=[qrk])

        P.replay([P.capture(lambda t=t: q_tile(t)) for t in range(ntl)])

        def emit_q_transposes():
          for t in range(ntl):
            qrot = self.qrot_t[t][:]
            qrk = "qrot%d" % t
            tb = self.bank("b")
            ptb = self.psum[tb][:].bitcast(BF16)
            for cb in range(2):
                for j in range(4):
                    P.pe(lambda h, cb=cb, j=j: h.transpose(out=ptb[:, (cb * 4 + j) * 128:(cb * 4 + j + 1) * 128],
                                                           in_=qrot[:, cb, j, :, :].rearrange("p h d -> p (h d)"),
                                                           identity=self.identb[:]),
                         reads=[qrk, "identb"], writes=[("ps", tb)])
            ptv = ptb[:, 0:1024].rearrange("p (a j q) -> p a j q", a=2, j=4)
            P.dve(lambda h, t=t: h.tensor_copy(out=QT[0:64, 0::2, t, :, :], in_=ptv[0:64]),
                  reads=[("ps", tb)], writes=["QT"])
            P.dve(lambda h, t=t: h.tensor_copy(out=QT[64:128, 1::2, t, :, :], in_=ptv[64:128]),
                  reads=[("ps", tb)], writes=["QT"])
        P.transfer(self.attn_keys, self.conv_keys)
        yT = self.yT
        for blk in range(2):
            wg, wgk = self.load_w_block(wbin, O_UG + blk * 512)
            wa, wak = self.load_w_block(wbin, O_UA + blk * 512)
            for cc in range(4):
                c = blk * 4 + cc
                bg = self.bank("a")
                pg = self.psum[bg]
                for k in range(8):
                    P.pe(lambda h, k=k, cc=cc, wg=wg, pg=pg: h.matmul(pg[:, 0:GE], lhsT=wg[:, k, cc * 128:(cc + 1) * 128],
                                                                      rhs=hT[:, k, :], start=(k == 0), stop=(k == 7)),
                         reads=[wgk, HK], writes=[("ps", bg)])
                si = self.rr("sg", 2)
                sg = self.sg[si]
                P.act(lambda h, pg=pg, sg=sg: h.activation(out=sg, in_=pg[:, 0:GE], func=AF.Sigmoid),
                      reads=[("ps", bg)], writes=["sg%d" % si])
                ba = self.bank("b")
                pa = self.psum[ba]
                for k in range(8):
                    P.pe(lambda h, k=k, cc=cc, wa=wa, pa=pa: h.matmul(pa[:, 0:GE], lhsT=wa[:, k, cc * 128:(cc + 1) * 128],
                                                                      rhs=hT[:, k, :], start=(k == 0), stop=(k == 7)),
                         reads=[wak, HK], writes=[("ps", ba)])
                P.dve(lambda h, c=c, pa=pa, sg=sg: h.tensor_tensor(out=yT[:, c, :], in0=pa[:, 0:GE], in1=sg, op=ALU.mult),
                      reads=[("ps", ba), "sg%d" % si], writes=["yT"])
        emit_q_transposes()
        if is_ctx:
            P.pool(lambda h: h.memset(yT[:, :, 0:HALO], 0.0), reads=["yT"], writes=["yT"])
            P.pool(lambda h: h.memset(yT[:, :, HALO + G:GE], 0.0), reads=["yT"], writes=["yT"])
        else:
            if lmask is not None:
                P.dve(lambda h: h.tensor_scalar(out=yT[:, :, 0:HALO], in0=yT[:, :, 0:HALO],
                                                scalar1=self.mask_sb[:, lmask:lmask + 1], scalar2=None, op0=ALU.mult),
                      reads=["yT", "mask_sb"], writes=["yT"])
            if rmask is not None:
                P.dve(lambda h: h.tensor_scalar(out=yT[:, :, HALO + G:GE], in0=yT[:, :, HALO + G:GE],
                                                scalar1=self.mask_sb[:, rmask:rmask + 1], scalar2=None, op0=ALU.mult),
                      reads=["yT", "mask_sb"], writes=["yT"])
        ycb = self.ycb
        s1b, s2b = 6, 7
        ps1, ps2 = self.psum[s1b], self.psum[s2b]
        for c in range(8):
            dgi = c % 2
            diag = self.diag[dgi]
            dgk = "diag%d" % dgi
            P.pool(lambda h, c=c: h.tensor_tensor(out=diag[:],
                                                  in0=self.identb[:].unsqueeze(1).to_broadcast([128, CK, 128]),
                                                  in1=self.cw[:, c, 0:CK].unsqueeze(2).to_broadcast([128, CK, 128]),
                                                  op=ALU.mult),
                   reads=["identb", "cw"], writes=[dgk])
            bc = self.bank("a")
            pc = self.psum[bc]
            for j in range(CK):
                P.pe(lambda h, j=j, c=c, pc=pc: h.matmul(pc[:, 0:G], lhsT=diag[:, j, :], rhs=yT[:, c, j:j + G],
                                                         start=(j == 0), stop=(j == CK - 1)),
                     reads=[dgk, "yT"], writes=[("ps", bc)])
            P.act(lambda h, c=c, pc=pc: h.activation(out=ycb[:, c, :], in_=pc[:, 0:G], func=AF.Identity,
                                                     bias=self.pcol[:, c, 4:5], scale=1.0),
                  reads=[("ps", bc), "pcol"], writes=[("ycb", c)])
            qi = self.rr("sq", 2)
            sq = self.sq[qi]
            P.act(lambda h, c=c, pc=pc, sq=sq: h.activation(out=sq, in_=pc[:, 0:G], func=AF.Square,
                                                            bias=self.pcol[:, c, 4:5], scale=1.0),
                  reads=[("ps", bc), "pcol"], writes=["sq%d" % qi])
            P.pe(lambda h, c=c: h.matmul(ps1[:, 0:G], lhsT=self.onesb[:], rhs=ycb[:, c, :], start=(c == 0), stop=(c == 7)),
                 reads=["onesb", ("ycb", c)], writes=[("ps", s1b)])
            P.pe(lambda h, c=c, sq=sq: h.matmul(ps2[:, 0:G], lhsT=self.onesb[:], rhs=sq, start=(c == 0), stop=(c == 7)),
                 reads=["onesb", "sq%d" % qi], writes=[("ps", s2b)])
        mean, rstd, nmr = self.mean, self.rstd, self.nmr
        P.dve(lambda h: h.tensor_scalar(out=mean, in0=ps1[:, 0:G], scalar1=1.0 / D, scalar2=None, op0=ALU.mult),
              reads=[("ps", s1b)], writes=["mean"])
        P.dve(lambda h: h.tensor_tensor(out=nmr, in0=mean, in1=mean, op=ALU.mult), reads=["mean"], writes=["nmr"])
        P.dve(lambda h: h.scalar_tensor_tensor(out=rstd, in0=ps2[:, 0:G], scalar=1.0 / D, in1=nmr,
                                               op0=ALU.mult, op1=ALU.subtract),
              reads=[("ps", s2b), "nmr"], writes=["rstd"])
        P.act(lambda h: h.activation(out=rstd, in_=rstd, func=AF.Ln, scale=1.0, bias=EPS), reads=["rstd"], writes=["rstd"])
        P.act(lambda h: h.activation(out=rstd, in_=rstd, func=AF.Exp, scale=-0.5), reads=["rstd"], writes=["rstd"])
        P.dve(lambda h: h.scalar_tensor_tensor(out=nmr, in0=mean, scalar=-1.0, in1=rstd, op0=ALU.mult, op1=ALU.mult),
              reads=["mean", "rstd"], writes=["nmr"])
        uT = self.uT
        for c in range(8):
            zi = self.rr("ztmp", 2)
            z = self.ztmp[zi]
            zk = "ztmp%d" % zi
            P.dve(lambda h, c=c, z=z: h.tensor_tensor(out=z, in0=ycb[:, c, :], in1=rstd, op=ALU.mult),
                  reads=[("ycb", c), "rstd"], writes=[zk])
            P.dve(lambda h, z=z: h.tensor_tensor(out=z, in0=z, in1=nmr, op=ALU.add), reads=[zk, "nmr"], writes=[zk])
            P.act(lambda h, c=c, z=z: h.activation(out=uT[:, c, :], in_=z, func=AF.Silu, scale=self.pcol[:, c, 5:6],
                                                   bias=self.pcol[:, c, 6:7]),
                  reads=[zk, "pcol"], writes=[("uT", c)])
        for blk in range(2):
            wga, wgak = self.load_w_block(wbin, O_GA + blk * 512)
            for cc in range(4):
                c = blk * 4 + cc
                bb = self.bank("a")
                pp = self.psum[bb]
                for k in range(8):
                    P.pe(lambda h, k=k, cc=cc, wga=wga, pp=pp: h.matmul(pp[:, 0:G], lhsT=wga[:, k, cc * 128:(cc + 1) * 128],
                                                                        rhs=hT[:, k, main], start=(k == 0), stop=(k == 7)),
                         reads=[wgak, HK], writes=[("ps", bb)])
                gi = self.rr("gat", 2)
                gat = self.gat[gi]
                P.act(lambda h, pp=pp, gat=gat: h.activation(out=gat, in_=pp[:, 0:G], func=AF.Silu),
                      reads=[("ps", bb)], writes=["gat%d" % gi])
                P.dve(lambda h, c=c, gat=gat: h.tensor_tensor(out=uT[:, c, :], in0=uT[:, c, :], in1=gat, op=ALU.mult),
                       reads=[("uT", c), "gat%d" % gi], writes=[("uT", c)])
        m1 = self.m1
        for blk in range(2):
            wco, wcok = self.load_w_block(self.wb_co[l], blk * 512)
            wma, wmak = self.load_w_block(wbin, O_MA + blk * 512)
            for cc in range(4):
                oc = blk * 4 + cc
                bm_ = self.bank("a")
                pmg = self.psum[bm_]
                for k in range(8):
                    P.pe(lambda h, k=k, cc=cc, wma=wma, pmg=pmg: h.matmul(pmg[:, 0:G], lhsT=wma[:, k, cc * 128:(cc + 1) * 128],
                                                                          rhs=hT[:, k, main], start=(k == 0), stop=(k == 7)),
                         reads=[wmak, HK], writes=[("ps", bm_)])
                zi = self.rr("ztmp", 2)
                z = self.ztmp[zi]
                zk = "ztmp%d" % zi
                P.act(lambda h, pmg=pmg, z=z: h.activation(out=z, in_=pmg[:, 0:G], func=AF.Sigmoid),
                      reads=[("ps", bm_)], writes=[zk])
                by = self.bank("b")
                py = self.psum[by]
                for c in range(8):
                    P.pe(lambda h, c=c, cc=cc, wco=wco, py=py: h.matmul(py[:, 0:G], lhsT=wco[:, c, cc * 128:(cc + 1) * 128],
                                                                        rhs=uT[:, c, :], start=(c == 0), stop=(c == 7)),
                         reads=[wcok, ("uT", c)], writes=[("ps", by)])
                P.dve(lambda h, oc=oc, py=py, z=z: h.tensor_tensor(out=m1[:, oc, :], in0=py[:, 0:G], in1=z, op=ALU.mult),
                      reads=[("ps", by), zk], writes=[("m1", oc)])

        P.transfer(self.conv_keys, self.attn_keys)
        QT = self.QT
        keytiles = [64, 65] if is_ctx else list(range(NKT))
        og = self.og
        nkt = len(keytiles)
        iters = [(hkv, t, ii, kt) for hkv in range(NKV) for t in range(ntl) for ii, kt in enumerate(keytiles)]
        SB = [1, 3, 4, 5]
        OB = [6, 2]
        LOOK = 3
        st_ = dict(gb={}, wgb={})
        st_["wgb"][0] = self.load_w_block(wbin, O_GB, 256)

        def emit_qk(n):
            hkv, t, ii, kt = iters[n]
            kvp, half = hkv // 2, hkv % 2
            pl = slice(64 * half, 64 * half + 64)
            bs = SB[n % 4]
            pS = self.psum[bs]
            qv = QT[:, hkv, t, :, :].rearrange("p j q -> p (j q)")
            P.pe(lambda h: h.matmul(pS[:], lhsT=self.KT[:, kvp, kt * 128:(kt + 1) * 128], rhs=qv, start=True, stop=True),
                 reads=[("KT", kt), "QT"], writes=[("ps", bs)])

        def emit_gate_b(hkv):
            gi = hkv % 2
            gbT = self.gbT[gi]
            gk_ = "gbT%d" % gi
            if hkv not in st_["wgb"]:
                st_["wgb"][hkv] = self.load_w_block(wbin, O_GB + hkv * 256, 256)
            wgb, wgbk = st_["wgb"].pop(hkv)
            if hkv + 1 < NKV:
                st_["wgb"][hkv + 1] = self.load_w_block(wbin, O_GB + (hkv + 1) * 256, 256)
            for j in range(4):
                bgb = self.bank("a1")
                pgb = self.psum[bgb]
                for k in range(8):
                    P.pe(lambda h, k=k: h.matmul(pgb[0:64, 0:G], lhsT=wgb[:, k, j * 64:(j + 1) * 64],
                                                 rhs=hT[:, k, main], start=(k == 0), stop=(k == 7)),
                         reads=[wgbk, HK], writes=[("ps", bgb)])
                P.act(lambda h: h.activation(out=gbT[0:64, j, :], in_=pgb[0:64, 0:G], func=AF.Silu),
                      reads=[("ps", bgb)], writes=[gk_])
            st_["gb"][hkv] = (gbT, gk_)

        def emit_epi1(blk):
            hkv, t = blk
            bo = OB[(hkv * ntl + t) % 2]
            po = self.psum[bo]
            oi = self.rr("oaug", 2)
            oaug = self.oaug[oi]
            ok_ = "oaug%d" % oi
            P.dve(lambda h: h.tensor_copy(out=oaug[0:HD + 1, :], in_=po[0:HD + 1, :]),
                  reads=[("ps", bo)], writes=[ok_])
            return (hkv, t, oaug, ok_)

        def emit_epi2(e):
            hkv, t, oaug, ok_ = e
            gbT, gk_ = st_["gb"][hkv]
            bb_ = 7
            pbc = self.psum[bb_]
            P.pe(lambda h: h.matmul(pbc[:], lhsT=self.sel[0:HD + 1, :], rhs=oaug[0:HD + 1, :], start=True, stop=True),
                 reads=["sel", ok_], writes=[("ps", bb_)])
            rbc = self.rbc
            P.dve(lambda h: h.reciprocal(out=rbc[0:64, :], in_=pbc[0:64, :]), reads=[("ps", bb_)], writes=["rbc"])
            P.dve(lambda h: h.tensor_tensor(out=oaug[0:64, :], in0=oaug[0:64, :], in1=rbc[0:64, :], op=ALU.mult),
                  reads=[ok_, "rbc"], writes=[ok_])
            P.dve(lambda h: h.tensor_tensor(out=og[0:64, 4 * hkv:4 * hkv + 4, t * 128:(t + 1) * 128],
                                            in0=oaug[0:64, :].rearrange("p (j q) -> p j q", j=4),
                                            in1=gbT[0:64, :, t * 128:(t + 1) * 128], op=ALU.mult),
                  reads=[ok_, gk_], writes=["og"])

        pending = []
        N = len(iters)
        sched = {}
        if next_steps:
            gap = max(1, min(24, (N - 8) // (2 * len(next_steps) + 1)))
            for si_, stp in enumerate(next_steps):
                sched[4 + 2 * si_ * gap] = ("s1", si_)
                sched[4 + (2 * si_ + 1) * gap] = ("s2", si_)
        st2s = {}
        for n in range(min(LOOK, N)):
            emit_qk(n)
        for n in range(N):
            hkv, t, ii, kt = iters[n]
            if t == 0 and ii == min(8, nkt - 1):
                emit_gate_b(hkv)
            bs = SB[n % 4]
            pS = self.psum[bs]
            pi = self.rr("PT", 4)
            PT = self.PT[pi]
            P.act(lambda h: h.activation(out=PT, in_=pS[:], func=AF.Exp, scale=SCALE),
                  reads=[("ps", bs)], writes=["PT%d" % pi])
            bo = OB[(hkv * ntl + t) % 2]
            po = self.psum[bo]
            vo = (kt * NKV + hkv) * (HD + 1)
            P.pe(lambda h: h.matmul(po[:, :], lhsT=self.Vflat[:, vo:vo + 128], rhs=PT,
                                    start=(ii == 0), stop=(ii == nkt - 1)),
                 reads=[("V", kt), "Vones", "PT%d" % pi], writes=[("ps", bo)])
            if n + LOOK < N:
                emit_qk(n + LOOK)
            while pending and pending[0][0] <= n:
                emit_epi2(pending.pop(0)[1])
            if ii == nkt - 1:
                pending.append((n + 3, emit_epi1((hkv, t))))
            if n in sched:
                kind, si_ = sched[n]
                if kind == "s1":
                    st2s[si_] = next_steps[si_]()
                else:
                    st2s.pop(si_)()
        while pending:
            emit_epi2(pending.pop(0)[1])
        if next_steps:
            for si_ in range(len(next_steps)):
                if si_ in st2s:
                    st2s.pop(si_)()
                elif not any(v == ("s1", si_) and k_ < N for k_, v in sched.items()):
                    next_steps[si_]()()
        wao_src = self.wb_ao[l]
        for blk in range(4):
            i, wt, wk = self.wslot()
            wv = wt[0:64, :, :].rearrange("p k c -> p (k c)").rearrange("p (h c) -> p h c", h=16)
            src = wao_src[blk]
            P.dma("sp", lambda h, wt=wt, src=src: h.dma_start(out=wt[0:64, :, :].rearrange("p k c -> p (k c)"), in_=src),
                  reads=[("wb", wao_src.name, blk)], writes=[wk])
            if blk % 2 == 0:
                wmb, wmbk = self.load_w_block(wbin, O_MB + (blk // 2) * 512)
            for o2 in range(2):
                oc = blk * 2 + o2
                cc = oc % 4
                bm_ = self.bank("a")
                pmg = self.psum[bm_]
                for k in range(8):
                    P.pe(lambda h, k=k, cc=cc, wmb=wmb, pmg=pmg: h.matmul(pmg[:, 0:G], lhsT=wmb[:, k, cc * 128:(cc + 1) * 128],
                                                                          rhs=hT[:, k, main], start=(k == 0), stop=(k == 7)),
                         reads=[wmbk, HK], writes=[("ps", bm_)])
                si = self.rr("sgb", 2)
                sgb = self.sgb[si]
                sk_ = "sgb%d" % si
                P.act(lambda h, pmg=pmg, sgb=sgb: h.activation(out=sgb, in_=pmg[:, 0:G], func=AF.Sigmoid),
                      reads=[("ps", bm_)], writes=[sk_])
                by = self.bank("b")
                py = self.psum[by]
                for hd in range(NH):
                    P.pe(lambda h, hd=hd, o2=o2, wv=wv, py=py: h.matmul(py[:, 0:G], lhsT=wv[:, hd, o2 * 128:(o2 + 1) * 128],
                                                                        rhs=og[0:64, hd, :], start=(hd == 0), stop=(hd == NH - 1)),
                         reads=[wk, "og"], writes=[("ps", by)])
                P.dve(lambda h, py=py, sgb=sgb: h.tensor_tensor(out=sgb, in0=py[:, 0:G], in1=sgb, op=ALU.mult),
                      reads=[("ps", by), sk_], writes=[sk_])
                P.dve(lambda h, oc=oc, sgb=sgb: h.tensor_tensor(out=m1[:, oc, :], in0=m1[:, oc, :], in1=sgb, op=ALU.add),
                       reads=[("m1", oc), sk_], writes=[("m1", oc)])
        wo = [self.load_w_block(self.wb_o[l], cb * 512) for cb in range(2)]
        def out_tile(t):
            if t % 2 == 1:
                self.rr("otmp", 2)
            xi = self.rr("xt", 2)
            xr = self.xt[xi]
            xk = "xt%d" % xi
            rows = x_src[t0 + t * 128:t0 + (t + 1) * 128, :]
            P.dma("pool", lambda h, xr=xr, rows=rows: h.dma_start(out=xr[:], in_=rows),
                  reads=[("dram", x_src.name, t0 + t * 128)], writes=[xk])
            si = self.rr("small", 8)
            sk = ("small", si)
            sm = self.small[:, si * 4:si * 4 + 4]
            pbs = []
            for cb in range(2):
                b_ = self.bank("q4")
                po_ = self.psum[b_]
                pbs.append((b_, po_))
                wot, wok = wo[cb]
                for c in range(8):
                    P.pe(lambda h, c=c, t=t, wot=wot, po_=po_: h.matmul(po_[:], lhsT=m1[:, c, t * 128:(t + 1) * 128],
                                                                        rhs=wot[:, c, :], start=(c == 0), stop=(c == 7)),
                         reads=[wok, ("m1", c)], writes=[("ps", b_)])
                P.act(lambda h, cb=cb, po_=po_, sm=sm: h.activation(out=self.junk[:, 0:512], in_=po_[:], func=AF.Square,
                                                                    accum_out=sm[:, cb:cb + 1]),
                      reads=[("ps", b_)], writes=["junk", sk])
            P.dve(lambda h, sm=sm: h.tensor_tensor(out=sm[:, 2:3], in0=sm[:, 0:1], in1=sm[:, 1:2], op=ALU.add),
                  reads=[sk], writes=[sk])
            P.act(lambda h, sm=sm: h.activation(out=sm[:, 3:4], in_=sm[:, 2:3], func=AF.Ln, scale=1.0 / D, bias=EPS),
                  reads=[sk], writes=[sk])
            P.act(lambda h, sm=sm: h.activation(out=sm[:, 2:3], in_=sm[:, 3:4], func=AF.Exp, scale=-0.5),
                  reads=[sk], writes=[sk])
            for cb in range(2):
                b_, po_ = pbs[cb]
                oi = self.rr("otmp", 2)
                ot = self.otmp[oi][:]
                otk = "otmp%d" % oi
                P.dve(lambda h, cb=cb, po_=po_, ot=ot: h.tensor_tensor(out=ot, in0=po_[:],
                                                                       in1=self.gtg[:, mset, cb * 512:(cb + 1) * 512], op=ALU.mult),
                      reads=[("ps", b_), "gtg"], writes=[otk])
                P.dve(lambda h, cb=cb, ot=ot, xr=xr, sm=sm: h.scalar_tensor_tensor(out=xr[:, cb * 512:(cb + 1) * 512], in0=ot,
                                                                                   scalar=sm[:, 2:3],
                                                                                   in1=xr[:, cb * 512:(cb + 1) * 512],
                                                                                   op0=ALU.mult, op1=ALU.add),
                      reads=[otk, sk, xk], writes=[xk])
            drow = t0 + t * 128
            if (not is_ctx) and x_dst.shape[0] < SEQ and drow >= x_dst.shape[0]:
                return
            dst = x_dst[drow:drow + 128, :]
            is_final = (l == self.layers[-1])
            idx = P.dma("pool", lambda h, xr=xr, dst=dst: h.dma_start(out=dst, in_=xr[:]), reads=[xk],
                        writes=[("dram", x_dst.name, drow)])

        P.replay([P.capture(lambda t=t: out_tile(t)) for t in range(ntl)])


import os
DBG_GROUPS = os.environ.get("KDBG_GROUPS")


def _make_builder(layers):
    ng = SEQ // G
    b_groups = {}
    a_tiles = {}
    for l in layers:
        lastl = (l == DEPTH - 1)
        if USE_CC:
            b_groups[l] = list(range(ng // 2))
            a_tiles[l] = list(range(HALF // 128))
        else:
            b_groups[l] = list(range(ng // 2)) if lastl else list(range(ng))
            a_tiles[l] = list(range(SEQ // 128))
    if DBG_GROUPS:
        for l in layers:
            b_groups[l] = [int(v) for v in DBG_GROUPS.split(",") if v != ""]
    ctx_upd = [l for l in layers if l != DEPTH - 1]
    out_rows = HALF if layers[-1] == DEPTH - 1 else SEQ
    return Builder(layers, b_groups, a_tiles, ctx_upd, out_rows)


def _rope_tables():
    n = SEQ
    row = np.repeat(np.arange(n // 64, dtype=np.float32), 64)
    col = np.tile(np.arange(64, dtype=np.float32), n // 64)
    inv = (10000.0 ** (-np.arange(0, 32, 2, dtype=np.float32) / 32.0)).astype(np.float32)
    ang = np.concatenate([row[:, None] * inv, col[:, None] * inv], axis=-1).astype(np.float32)
    return np.concatenate([np.cos(ang), np.sin(ang)], axis=-1).astype(np.float32)


def _perm_rows(a, half):
    if half == 0:
        return np.ascontiguousarray(a)
    return np.ascontiguousarray(np.concatenate([a[HALF:], a[:HALF]], axis=0))


LAUNCH_LAYERS = [[0, 1]]
USE_CC = True
RG_PAIRS = [[0, 1], [2, 3], [4, 5], [6, 7]]


def kernel(x, c, ctx, c_ctx, w_mod, b_mod, g_pre, g_post, w_in, conv_w, conv_b,
           ln_g, ln_b, w_conv_out, q_norm_g, k_norm_g, w_attn_out, w_out):
    f = lambda a: np.ascontiguousarray(np.asarray(a, dtype=np.float32))
    x, c, ctx, c_ctx = f(x), f(c), f(ctx), f(c_ctx)
    shared = dict(w_mod=f(w_mod), b_mod=f(b_mod), g_pre=f(g_pre), g_post=f(g_post), w_in=f(w_in),
                  conv_w=f(conv_w), conv_b=f(conv_b), ln_g=f(ln_g), ln_b=f(ln_b), w_conv_out=f(w_conv_out),
                  q_norm_g=f(q_norm_g), k_norm_g=f(k_norm_g), w_attn_out=f(w_attn_out), w_out=f(w_out))
    rope = _rope_tables()
    xs = []
    cs = []
    for core in range(8):
        b, half = core // 2, core % 2
        xs.append(_perm_rows(x[b], half))
        cs.append(ctx[b])
    for layers in LAUNCH_LAYERS:
        bld = _make_builder(layers)
        bld.build()
        in_maps = []
        for core in range(8):
            b, half = core // 2, core % 2
            m = dict(shared)
            m["x"] = xs[core]
            m["ctx"] = cs[core]
            m["cvec"] = np.ascontiguousarray(np.stack([c[b], c_ctx], axis=0))
            m["rope"] = _perm_rows(rope, half)
            mk = np.zeros((128, 2), np.float32)
            mk[:, 0] = 1.0 if half == 1 else 0.0
            mk[:, 1] = 1.0 if half == 0 else 0.0
            m["masks"] = mk
            in_maps.append(m)
        res = run_bass_kernel_spmd(bld.nc, in_maps, core_ids=list(range(8)))
        outs = [r["out"] for r in res.results]
        if layers[-1] != DEPTH - 1:
            xs = [np.asarray(o, dtype=np.float32) for o in outs]
            cs = [np.asarray(r["ctx_out"], dtype=np.float32) for r in res.results]
    out = np.zeros((NB, SEQ, D), np.float32)
    for core in range(8):
        b, half = core // 2, core % 2
        out[b, half * HALF:(half + 1) * HALF] = np.asarray(outs[core], dtype=np.float32)
    return out
```
